# Optimizing a Trainium2 kernel written in Bass

```python
import jax, jax.numpy as jnp
from jax import lax
import numpy as np

D_MODEL = 1024
BATCH = 16
SEQ = 256
DEPTH = 4
DEC_BATCH = 2
DEC_SEQ = 2048
PAST_LEN = 256

GRID_W = 64
N_MIXERS = 3
CONV_WIDTH = 3
HEAD_DIM = 128
N_HEADS = D_MODEL // HEAD_DIM
N_KV_HEADS = 2
ROPE_THETA = 10000.0
Q_BLOCK = 128
GLA_HEADS = 4
GLA_DK = (D_MODEL // 2) // GLA_HEADS
GLA_DV = D_MODEL // GLA_HEADS
GLA_GATE_RANK = 16
GLA_TAU = 16.0
GLA_CHUNK = 64
D_FF = -(-8 * D_MODEL // (3 * 256)) * 256
DEEPNORM_ALPHA = (2.0 * DEPTH) ** 0.25
DEEPNORM_BETA = (8.0 * DEPTH) ** -0.25
LN_EPS = 1e-5
RMS_EPS = 1e-6
N_CONV_LAYERS = len(range(0, DEPTH, N_MIXERS))
N_ATTN_LAYERS = len(range(1, DEPTH, N_MIXERS))
N_GLA_LAYERS = len(range(2, DEPTH, N_MIXERS))

kernel_name = "hybrid_diffusion_conv_gqa_gla_step"


def layer_norm(x, g, b):
    xf = x.astype(jnp.float32)
    mu = jnp.mean(xf, axis=-1, keepdims=True)
    var = jnp.mean(jnp.square(xf - mu), axis=-1, keepdims=True)
    return ((xf - mu) * lax.rsqrt(var + LN_EPS) * g + b).astype(x.dtype)


def rms_norm(x, g):
    xf = x.astype(jnp.float32)
    return (xf * lax.rsqrt(jnp.mean(xf * xf, axis=-1, keepdims=True) + RMS_EPS) * g).astype(x.dtype)


def modulation(cond, w, b):
    m = (jax.nn.silu(cond) @ w + b)[:, None, :]
    return jnp.split(m, 6, axis=-1)


def short_conv(h, w_in, w_conv, w_out):
    bg, cg, u = jnp.split(h @ w_in, 3, axis=-1)
    u = cg * u
    up = jnp.pad(u, ((0, 0), (1, 1), (0, 0)))
    y = up[:, :-2] * w_conv[0] + up[:, 1:-1] * w_conv[1] + up[:, 2:] * w_conv[2]
    return (bg * y) @ w_out


def axial_rope_tables(seq):
    rows = seq // GRID_W
    row = jnp.repeat(jnp.arange(rows), GRID_W)
    col = jnp.tile(jnp.arange(GRID_W), rows)
    pos = jnp.stack([row, col], axis=-1).astype(jnp.float32)
    n_freq = HEAD_DIM // 4
    freqs = ROPE_THETA ** (-jnp.arange(n_freq, dtype=jnp.float32) / n_freq)
    ang = pos[:, :, None] * freqs
    return jnp.cos(ang), jnp.sin(ang)


def apply_axial_rope(x, cos, sin):
    B, S, Hh, _ = x.shape
    xr = x.astype(jnp.float32).reshape(B, S, Hh, 2, 2, HEAD_DIM // 4)
    x1, x2 = xr[..., 0, :], xr[..., 1, :]
    cb, sb = cos[None, :, None], sin[None, :, None]
    out = jnp.stack([x1 * cb - x2 * sb, x2 * cb + x1 * sb], axis=-2)
    return out.reshape(B, S, Hh, HEAD_DIM).astype(x.dtype)


def attn_qkv(h, w_qkv, q_g, k_g):
    B, S, _ = h.shape
    q, k, v = jnp.split(h @ w_qkv, [N_HEADS * HEAD_DIM, (N_HEADS + N_KV_HEADS) * HEAD_DIM], axis=-1)
    q = rms_norm(q.reshape(B, S, N_HEADS, HEAD_DIM), q_g)
    k = rms_norm(k.reshape(B, S, N_KV_HEADS, HEAD_DIM), k_g)
    v = v.reshape(B, S, N_KV_HEADS, HEAD_DIM)
    return q, k, v


def blocked_attention(q, k, v):
    B, S, H, hd = q.shape
    KV = k.shape[2]
    G = H // KV
    nb = S // Q_BLOCK
    qb = q.reshape(B, nb, Q_BLOCK, KV, G, hd).transpose(1, 0, 2, 3, 4, 5)
    scale = HEAD_DIM ** -0.5

    def one_block(qblk):
        s = jnp.einsum('bqkgd,btkd->bkgqt', qblk, k).astype(jnp.float32) * scale
        p = jax.nn.softmax(s, axis=-1).astype(v.dtype)
        return jnp.einsum('bkgqt,btkd->bqkgd', p, v)

    o = lax.map(one_block, qb)
    return o.transpose(1, 0, 2, 3, 4, 5).reshape(B, S, H * hd)


def gla_log_gate(h, w1, w2, b):
    B, S, _ = h.shape
    z = ((h @ w1) @ w2 + b).astype(jnp.float32)
    return (jax.nn.log_sigmoid(z) / GLA_TAU).reshape(B, S, GLA_HEADS, GLA_DK)


def gla_chunk_scan(q, k, v, logg, s0):
    B, S, H, _ = q.shape
    L = GLA_CHUNK
    n = S // L

    def to_chunks(a):
        return a.astype(jnp.float32).reshape(B, n, L, H, a.shape[-1]).transpose(1, 0, 3, 2, 4)

    mask = jnp.tril(jnp.ones((L, L), dtype=bool))[:, :, None]

    def step(state, inp):
        qc, kc, vc, gc = inp
        bcum = jnp.cumsum(gc, axis=2)
        o_inter = jnp.einsum('bhld,bhde->bhle', qc * jnp.exp(bcum), state)
        diff = bcum[:, :, :, None, :] - bcum[:, :, None, :, :]
        decay = jnp.where(mask, jnp.exp(jnp.where(mask, diff, 0.0)), 0.0)
        a = jnp.einsum('bhtd,bhsd,bhtsd->bhts', qc, kc, decay)
        o_intra = jnp.einsum('bhts,bhse->bhte', a, vc)
        b_last = bcum[:, :, -1]
        new_state = jnp.exp(b_last)[..., None] * state + jnp.einsum(
            'bhsd,bhse->bhde', kc * jnp.exp(b_last[:, :, None] - bcum), vc)
        return new_state, o_inter + o_intra

    s_fin, o = lax.scan(step, s0.astype(jnp.float32),
                        (to_chunks(q), to_chunks(k), to_chunks(v), to_chunks(logg)))
    o = o.transpose(1, 0, 3, 2, 4).reshape(B, S, H, v.shape[-1])
    return o.astype(v.dtype), s_fin


def gla_mixer(h, s0f, s0b, w_in, w_g1, w_g2, b_g, norm_g, w_o):
    B, S, _ = h.shape
    hk = GLA_HEADS * GLA_DK
    hv = GLA_HEADS * GLA_DV
    q, k, v, og = jnp.split(h @ w_in, [hk, 2 * hk, 2 * hk + hv], axis=-1)
    q = q.reshape(B, S, GLA_HEADS, GLA_DK) * (GLA_DK ** -0.5)
    k = k.reshape(B, S, GLA_HEADS, GLA_DK)
    v = v.reshape(B, S, GLA_HEADS, GLA_DV)
    lg_f = gla_log_gate(h, w_g1[0], w_g2[0], b_g[0])
    lg_b = gla_log_gate(h, w_g1[1], w_g2[1], b_g[1])
    o_f, sf = gla_chunk_scan(q, k, v, lg_f, s0f)
    o_b, sb = gla_chunk_scan(jnp.flip(q, 1), jnp.flip(k, 1), jnp.flip(v, 1), jnp.flip(lg_b, 1), s0b)
    o = o_f + jnp.flip(o_b, 1)
    o = rms_norm(o, norm_g) * jax.nn.silu(og).reshape(B, S, GLA_HEADS, GLA_DV)
    return o.reshape(B, S, hv) @ w_o, sf, sb


def swiglu(h, w_in, w_out):
    g, u = jnp.split(h @ w_in, 2, axis=-1)
    return (jax.nn.silu(g) * u) @ w_out


def setup_inputs(seed: int = 0) -> dict:
    key = jax.random.key(seed)
    ks = jax.random.split(key, 32)
    f32 = jnp.float32
    D = D_MODEL

    def nrm(k, shape, scale=1.0):
        return jax.random.normal(k, shape, f32) * scale

    qkv_out = (N_HEADS + 2 * N_KV_HEADS) * HEAD_DIM
    gla_in = 2 * GLA_HEADS * GLA_DK + 2 * GLA_HEADS * GLA_DV
    return {
        "x_prompt": nrm(ks[0], (BATCH, SEQ, D)),
        "x_sample": nrm(ks[1], (DEC_BATCH, DEC_SEQ, D)),
        "c": nrm(ks[2], (DEC_BATCH, D)),
        "cache_k": nrm(ks[3], (DEC_BATCH, N_ATTN_LAYERS, PAST_LEN, N_KV_HEADS, HEAD_DIM)),
        "cache_v": nrm(ks[4], (DEC_BATCH, N_ATTN_LAYERS, PAST_LEN, N_KV_HEADS, HEAD_DIM)),
        "state_gla_fwd": nrm(ks[5], (DEC_BATCH, N_GLA_LAYERS, GLA_HEADS, GLA_DK, GLA_DV), 0.5),
        "state_gla_bwd": nrm(ks[6], (DEC_BATCH, N_GLA_LAYERS, GLA_HEADS, GLA_DK, GLA_DV), 0.5),
        "c_ctx": nrm(ks[7], (D,)),
        "w_ada": nrm(ks[8], (DEPTH, D, 6 * D), 0.5 * D ** -0.5),
        "b_ada": nrm(ks[9], (DEPTH, 6 * D), 0.02),
        "ln_g": 1.0 + nrm(ks[10], (DEPTH, 2, D), 0.02),
        "ln_b": nrm(ks[11], (DEPTH, 2, D), 0.02),
        "conv_w_in": nrm(ks[12], (N_CONV_LAYERS, D, 3 * D), D ** -0.5),
        "conv_w": nrm(ks[13], (N_CONV_LAYERS, CONV_WIDTH, D), CONV_WIDTH ** -0.5),
        "conv_w_out": nrm(ks[14], (N_CONV_LAYERS, D, D), DEEPNORM_BETA * D ** -0.5),
        "attn_w_qkv": nrm(ks[15], (N_ATTN_LAYERS, D, qkv_out), D ** -0.5),
        "attn_q_norm": 1.0 + nrm(ks[16], (N_ATTN_LAYERS, HEAD_DIM), 0.02),
        "attn_k_norm": 1.0 + nrm(ks[17], (N_ATTN_LAYERS, HEAD_DIM), 0.02),
        "attn_w_o": nrm(ks[18], (N_ATTN_LAYERS, N_HEADS * HEAD_DIM, D), DEEPNORM_BETA * (N_HEADS * HEAD_DIM) ** -0.5),
        "gla_w_in": nrm(ks[19], (N_GLA_LAYERS, D, gla_in), D ** -0.5),
        "gla_w_gate1": nrm(ks[20], (N_GLA_LAYERS, 2, D, GLA_GATE_RANK), D ** -0.5),
        "gla_w_gate2": nrm(ks[21], (N_GLA_LAYERS, 2, GLA_GATE_RANK, GLA_HEADS * GLA_DK), GLA_GATE_RANK ** -0.5),
        "gla_b_gate": nrm(ks[22], (N_GLA_LAYERS, 2, GLA_HEADS * GLA_DK), 0.02),
        "gla_norm": 1.0 + nrm(ks[23], (N_GLA_LAYERS, GLA_DV), 0.02),
        "gla_w_o": nrm(ks[24], (N_GLA_LAYERS, GLA_HEADS * GLA_DV, D), DEEPNORM_BETA * (GLA_HEADS * GLA_DV) ** -0.5),
        "ffn_w_in": nrm(ks[25], (DEPTH, D, 2 * D_FF), D ** -0.5),
        "ffn_w_out": nrm(ks[26], (DEPTH, D_FF, D), DEEPNORM_BETA * D_FF ** -0.5),
    }


def reference(x_prompt, x_sample, c, cache_k, cache_v, state_gla_fwd, state_gla_bwd, c_ctx,
              w_ada, b_ada, ln_g, ln_b, conv_w_in, conv_w, conv_w_out,
              attn_w_qkv, attn_q_norm, attn_k_norm, attn_w_o,
              gla_w_in, gla_w_gate1, gla_w_gate2, gla_b_gate, gla_norm, gla_w_o,
              ffn_w_in, ffn_w_out):
    xp = x_prompt
    xs = x_sample
    cos, sin = axial_rope_tables(xs.shape[1])
    cond_ctx = c_ctx[None, :]
    new_k, new_v, new_sf, new_sb = [], [], [], []
    for i in range(DEPTH):
        kind = i % N_MIXERS
        j = i // N_MIXERS
        sh_p, sc_p, ga_p, sh2_p, sc2_p, ga2_p = modulation(cond_ctx, w_ada[i], b_ada[i])
        sh_s, sc_s, ga_s, sh2_s, sc2_s, ga2_s = modulation(c, w_ada[i], b_ada[i])
        hp = xp * (1.0 + sc_p) + sh_p
        hs = xs * (1.0 + sc_s) + sh_s
        if kind == 0:
            mix_p = short_conv(hp, conv_w_in[j], conv_w[j], conv_w_out[j])
            mix_s = short_conv(hs, conv_w_in[j], conv_w[j], conv_w_out[j])
        elif kind == 1:
            qp, kp, vp = attn_qkv(hp, attn_w_qkv[j], attn_q_norm[j], attn_k_norm[j])
            mix_p = blocked_attention(qp, kp, vp) @ attn_w_o[j]
            new_k.append(kp)
            new_v.append(vp)
            qs, ks_, vs = attn_qkv(hs, attn_w_qkv[j], attn_q_norm[j], attn_k_norm[j])
            qs = apply_axial_rope(qs, cos, sin)
            ks_ = apply_axial_rope(ks_, cos, sin)
            k_all = jnp.concatenate([cache_k[:, j].astype(ks_.dtype), ks_], axis=1)
            v_all = jnp.concatenate([cache_v[:, j].astype(vs.dtype), vs], axis=1)
            mix_s = blocked_attention(qs, k_all, v_all) @ attn_w_o[j]
        else:
            zeros = jnp.zeros((xp.shape[0], GLA_HEADS, GLA_DK, GLA_DV), jnp.float32)
            mix_p, sf, sb = gla_mixer(hp, zeros, zeros, gla_w_in[j], gla_w_gate1[j], gla_w_gate2[j],
                                      gla_b_gate[j], gla_norm[j], gla_w_o[j])
            new_sf.append(sf)
            new_sb.append(sb)
            mix_s, _, _ = gla_mixer(hs, state_gla_fwd[:, j], state_gla_bwd[:, j], gla_w_in[j],
                                    gla_w_gate1[j], gla_w_gate2[j], gla_b_gate[j], gla_norm[j], gla_w_o[j])
        xp = layer_norm(DEEPNORM_ALPHA * xp + ga_p * mix_p, ln_g[i, 0], ln_b[i, 0])
        xs = layer_norm(DEEPNORM_ALPHA * xs + ga_s * mix_s, ln_g[i, 0], ln_b[i, 0])
        hp = xp * (1.0 + sc2_p) + sh2_p
        hs = xs * (1.0 + sc2_s) + sh2_s
        xp = layer_norm(DEEPNORM_ALPHA * xp + ga2_p * swiglu(hp, ffn_w_in[i], ffn_w_out[i]), ln_g[i, 1], ln_b[i, 1])
        xs = layer_norm(DEEPNORM_ALPHA * xs + ga2_s * swiglu(hs, ffn_w_in[i], ffn_w_out[i]), ln_g[i, 1], ln_b[i, 1])
    new_cache_k = jnp.stack(new_k, axis=1)
    new_cache_v = jnp.stack(new_v, axis=1)
    new_state_fwd = jnp.stack(new_sf, axis=1)
    new_state_bwd = jnp.stack(new_sb, axis=1)
    return (xp, xs, new_cache_k, new_cache_v, new_state_fwd, new_state_bwd)
```

```python
import contextlib
import numpy as np
import concourse.bass as bass
import concourse.mybir as mybir

F32 = mybir.dt.float32
BF16 = mybir.dt.bfloat16
AF = mybir.ActivationFunctionType
ALU = mybir.AluOpType
AX = mybir.AxisListType

ENGS = ("pe", "act", "dve", "pool", "sp")
STRICT = [True]
_DT_SIZE = {}


def _dsize(dt):
    s = str(dt)
    if "32" in s:
        return 4
    if "16" in s:
        return 2
    if "64" in s:
        return 8
    return 1


def region(ap):
    pairs = [tuple(x) for x in ap.ap]
    es = _dsize(ap.dtype)
    off = int(ap.offset)
    name = ap.name
    if str(ap.space) == "PSUM":
        return (name, 0, 128, 0, 1 << 20)
    if str(ap.space) in ("SB", "PSUM"):
        ps, pc = pairs[0]
        if ps == 0:
            ps = 1 << 40
        p0 = off // ps if ps < (1 << 40) else 0
        fo = off - p0 * ps if ps < (1 << 40) else off
        ext = 0
        for st, cnt in pairs[1:]:
            ext += abs(st) * (cnt - 1)
        return (name, p0, p0 + pc, fo * es, (fo + ext + 1) * es)
    ext = 0
    for st, cnt in pairs:
        ext += abs(st) * (cnt - 1)
    return (name, 0, 1, off * es, (off + ext + 1) * es)


class Op:
    __slots__ = ("eng", "fn", "deps", "flag", "k", "is_dma", "done", "idx", "is_cc")

    def __init__(self, eng, fn, is_dma):
        self.eng = eng
        self.fn = fn
        self.deps = []
        self.flag = False
        self.k = 0
        self.is_dma = is_dma
        self.done = None
        self.idx = 0
        self.is_cc = False


class Rec:
    __slots__ = ("p0", "p1", "lo", "hi", "writer", "readers", "pseudo")

    def __init__(self, p0, p1, lo, hi, writer, pseudo=False):
        self.p0, self.p1, self.lo, self.hi = p0, p1, lo, hi
        self.writer = writer
        self.readers = []
        self.pseudo = pseudo


class Prog:
    def __init__(self, nc, n_dma_sems=40):
        self.nc = nc
        self.ops = {e: [] for e in ENGS}
        self.recs = {}
        self.stack = contextlib.ExitStack()
        self.esem = {e: self.stack.enter_context(nc.semaphore("s_" + e)) for e in ENGS}
        self.dsems = [self.stack.enter_context(nc.semaphore("d%d" % i)) for i in range(n_dma_sems + 1)]
        self.cc_idx = n_dma_sems
        self.dval = [0] * (n_dma_sems + 1)
        self.dlast = [None] * (n_dma_sems + 1)
        self.n_dma = n_dma_sems
        self.dnext = 0
        self.dnext_pool = 0
        self.out_dmas = []
        self.nops = 0

    def sbuf(self, name, shape, dt):
        return self.stack.enter_context(self.nc.sbuf_tensor(name, list(shape), dt))

    def psum(self, name, shape, dt=F32):
        return self.stack.enter_context(self.nc.psum_tensor(name, list(shape), dt))

    def op(self, eng, fn, reads=(), writes=(), dma=False, is_out=False, same_ok=False, cc=False):
        o = Op(eng, fn, dma)
        o.idx = self.nops
        self.nops += 1
        deps = []
        reads = list(reads)
        writes = [(ap, False) for ap in writes]
        for ap in list(reads):
            if str(ap.space) == "PSUM":
                reads.remove(ap)
                writes.append((ap, True))
        for ap in reads:
            name, p0, p1, lo, hi = region(ap)
            for r in self.recs.get(name, ()):
                if r.p0 < p1 and p0 < r.p1 and r.lo < hi and lo < r.hi:
                    if r.writer is not None:
                        deps.append((r.writer, "raw"))
                    if not dma:
                        r.readers = [x for x in r.readers if x.is_dma or x.eng != eng]
                    r.readers.append(o)
        for ap, pseudo in writes:
            name, p0, p1, lo, hi = region(ap)
            lst = self.recs.setdefault(name, [])
            keep = []
            for r in lst:
                if r.p0 < p1 and p0 < r.p1 and r.lo < hi and lo < r.hi:
                    if r.writer is not None:
                        deps.append((r.writer, "rar" if (pseudo and r.pseudo) else ("raw" if pseudo else "waw")))
                    for x in r.readers:
                        if x is not o:
                            deps.append((x, "war"))
                    if p0 <= r.p0 and r.p1 <= p1 and lo <= r.lo and r.hi <= hi:
                        continue
                keep.append(r)
            keep.append(Rec(p0, p1, lo, hi, o, pseudo))
            self.recs[name] = keep
        for d, kind in deps:
            if d is o:
                continue
            if (not d.is_dma) and (not dma) and d.eng == eng:
                if eng == "pe" or same_ok or kind == "rar" or (kind != "raw" and not STRICT[0]):
                    continue
            o.deps.append(d)
            if not d.is_dma:
                d.flag = True
        if dma:
            half = self.n_dma // 2
            if cc:
                i = self.cc_idx
                o.is_cc = True
            elif eng == "pool":
                i = half + self.dnext_pool
                self.dnext_pool = (self.dnext_pool + 1) % (self.n_dma - half)
            else:
                i = self.dnext
                self.dnext = (self.dnext + 1) % half
            if self.dlast[i] is not None:
                o.deps.append(self.dlast[i])
            self.dval[i] += 1 if cc else 16
            o.done = (i, self.dval[i])
            self.dlast[i] = o
            if is_out:
                self.out_dmas.append(o)
        self.ops[eng].append(o)
        return o

    def emit(self):
        nc = self.nc
        for e in ENGS:
            k = 0
            for o in self.ops[e]:
                if o.flag and not o.is_dma:
                    k += 1
                    o.k = k
        final_waits = [o.done for o in self.out_dmas]
        engmap = {"pe": "tensor", "act": "scalar", "dve": "vector", "pool": "gpsimd", "sp": "sync"}
        with nc.Block() as block:
            for e in ENGS:
                def body(engine, e=e):
                    waited = {}
                    for o in self.ops[e]:
                        need = {}
                        for d in o.deps:
                            if d.is_dma:
                                key = ("d", d.done[0])
                                v = d.done[1]
                            else:
                                key = ("e", d.eng)
                                v = d.k
                            if v > need.get(key, 0):
                                need[key] = v
                        for key, v in need.items():
                            if waited.get(key, 0) >= v:
                                continue
                            waited[key] = v
                            sem = self.dsems[key[1]] if key[0] == "d" else self.esem[key[1]]
                            engine.wait_ge(sem, v)
                        ins = o.fn(engine)
                        if o.is_dma:
                            if o.is_cc:
                                ins.then_inc(self.dsems[o.done[0]])
                            else:
                                ins.then_inc(self.dsems[o.done[0]], 16)
                        elif o.flag:
                            ins.then_inc(self.esem[e], 1)
                    if e == "sp":
                        for (i, v) in final_waits:
                            if waited.get(("d", i), 0) < v:
                                waited[("d", i)] = v
                                engine.wait_ge(self.dsems[i], v)
                getattr(block, engmap[e])(body)
        self.stack.close()

from concourse.bass_utils import run_bass_kernel_spmd

D = 1024
DFF = 2816
NG = 2
NCORES = [8]
GROUPS = [[[0, 1, 2, 3], [4, 5, 6, 7]]]
TOK = NG * 512
DEPTH = 4
ALPHA = (2.0 * DEPTH) ** 0.25
LN_EPS = 1e-5
RMS_EPS = 1e-6
NSLOT = 6
DEBUG = [False]
MARKS = []


class K:
    pass


def build(nlayers=DEPTH, nsub=None, stage=None):
    nc = bass.Bass("TRN2", target_bir_lowering=False)
    k = K()

    def din(name, shape, dt=F32):
        return nc.dram_tensor(name, list(shape), dt, kind="ExternalInput").ap()

    def dout(name, shape, dt=F32):
        return nc.dram_tensor(name, list(shape), dt, kind="ExternalOutput").ap()

    xin = din("xin", [TOK, D])
    NC_ = 4
    CPC = 48 // NC_
    cnd = din("cnd", [3, D])
    ohb_d = din("ohb", [128, 2])
    ck = din("ck", [256, 256])
    cv = din("cv", [256, 256])
    s0f = din("s0f", [512, 256])
    s0b = din("s0b", [512, 256])
    w_ada_s = din("w_ada_s", [4, D, CPC * 128])
    b_ada_s = din("b_ada_s", [4 * CPC, 128])
    mod_in = nc.dram_tensor("mod_in", [128, 256], F32)
    mod_out = nc.dram_tensor("mod_out", [128 * NC_, 256], F32)
    ln_g = din("ln_g", [4, 2, D])
    ln_b = din("ln_b", [4, 2, D])
    conv_w_in = din("conv_w_in", [2, D, 3 * D])
    conv_w = din("conv_w", [2, 3, D])
    conv_w_out = din("conv_w_out", [2, D, D])
    attn_w_qkv = din("attn_w_qkv", [1, D, 1536])
    attn_q_norm = din("attn_q_norm", [1, 128])
    attn_k_norm = din("attn_k_norm", [1, 128])
    attn_w_o = din("attn_w_o", [1, D, D])
    gla_w_in = din("gla_w_in", [1, D, 3072])
    gla_w_gate1 = din("gla_w_gate1", [1, 2, D, 16])
    gla_w_gate2 = din("gla_w_gate2", [1, 2, 16, 512])
    gla_b_gate = din("gla_b_gate", [1, 2, 512])
    gla_norm = din("gla_norm", [1, 256])
    gla_w_o = din("gla_w_o", [1, D, D])
    ffn_w_in = din("ffn_w_in", [4, D, 2 * DFF])
    ffn_w_out = din("ffn_w_out", [4, DFF, D])
    ident_d = din("ident", [128, 128])
    rcos = din("rcos", [512, 64])
    rsin = din("rsin", [512, 64])
    sel_d = din("sel", [8, 2])
    hflag_d = din("hflag", [128, 2])
    oh4_d = din("oh4", [128, 4])
    trif_d = din("trif", [128, 128])
    trib_d = din("trib", [128, 128])

    y = dout("y", [TOK, D])
    nk = dout("nk", [512, 256])
    nv = dout("nv", [512, 256])
    nsf = dout("nsf", [1024, 256])
    nsb = dout("nsb", [1024, 256])

    XA = XB = None
    SIN = dout("SIN", [2, 512, 256])
    bnd_in = nc.dram_tensor("bnd_in", [2, D], F32)
    bnd_out = nc.dram_tensor("bnd_out", [8, D], F32)
    kv_in = nc.dram_tensor("kv_in", [128, 2048], BF16)
    kv_out = nc.dram_tensor("kv_out", [512, 2048], BF16)
    gl_in = nc.dram_tensor("gl_in", [1024, 256], F32)
    gl_out = nc.dram_tensor("gl_out", [4096, 256], F32)
    gd_in = nc.dram_tensor("gd_in", [4, 256], F32)
    gd_out = nc.dram_tensor("gd_out", [16, 256], F32)
    DBG = dout("DBG", [128, 4096]) if DEBUG[0] else None
    DBGB = [dout("DBGB%d" % i, [128, 4096], BF16) for i in range(4)] if DEBUG[0] else None

    P = Prog(nc, n_dma_sems=48)
    op = P.op

    xg = [P.sbuf("xg%d" % i, [128, 4, D], F32) for i in range(2)]
    xb0_ = P.sbuf("xb0", [128, D], BF16)
    xb = [xb0_, xb0_]
    misc1 = P.sbuf("misc1", [128, D], F32)
    cnds = misc1[0:3, :]
    misc2 = P.sbuf("misc2", [128, 512], F32)
    xh = misc1[0:2, :]
    xhb = misc2[0:2, :].bitcast(BF16)
    hT = P.sbuf("hT", [128, 8, 514], BF16)
    hT2 = P.sbuf("hT2", [128, 8, 512], BF16)
    slots = [P.sbuf("slot%d" % i, [128, 8, 512], BF16) for i in range(NSLOT)]
    ARENA_F = 16128
    arena = P.sbuf("arena", [128, ARENA_F], F32)
    gbc = P.sbuf("gbc", [128, 2, 2, D], F32)
    lnbc = P.sbuf("lnbc", [128, 2, D], F32)
    modT = P.sbuf("modT", [128, 48, 2], F32)
    badr = P.sbuf("badr", [48, 128], F32)
    silT = P.sbuf("silT", [128, 8, 3], BF16)
    modA = P.sbuf("modA", [128, 4, 144], F32)
    modpf = P.sbuf("modpf", [128, 256], F32)
    modp = modpf[:, 0:4 * CPC * 3].rearrange("p (a b) -> p a b", b=3)
    onesf = P.sbuf("onesf", [128, 128], F32)
    dg = [P.sbuf("dg%d" % i, [128, 128], F32) for i in range(2)]
    idf = P.sbuf("idf", [128, 128], F32)
    idb = P.sbuf("idb", [128, 128], BF16)
    onesb = P.sbuf("onesb", [128, 128], BF16)
    w1buf = P.sbuf("w1buf", [128, 8, 64], BF16)
    lnst = P.sbuf("lnst", [128, 64], F32)
    small = P.sbuf("small", [128, 72], F32)
    sgt = [P.sbuf("sgt%d" % i, [128, 512], F32) for i in range(2)]
    ttmp = [P.sbuf("ttmp%d" % i, [128, 512], F32) for i in range(2)]
    cwr = P.sbuf("cwr", [24, 128], F32)
    cwT = P.sbuf("cwT", [128, 24], F32)
    PS = [P.psum("ps%d" % i, [128, 512], F32) for i in range(8)]

    def carve(off_bytes, shape, dt):
        n = 1
        for s in shape[1:]:
            n *= s
        nb = n * _dsize(dt)
        assert off_bytes % 4 == 0 and nb % 4 == 0 and off_bytes + nb <= ARENA_F * 4, (off_bytes, nb)
        v = arena[:, off_bytes // 4:(off_bytes + nb) // 4]
        if dt != F32:
            v = v.bitcast(dt)
        if len(shape) == 3:
            v = v.rearrange("p (a b) -> p a b", b=shape[2])
        elif len(shape) == 4:
            v = v.rearrange("p (a b c) -> p a b c", b=shape[2], c=shape[3])
        return v

    st = {"slot": 0, "acc": 0, "sub": 0, "ffn_g0_ready": False}
    HB = [hT2, hT]
    cur = {"h": hT}

    def pre_ffn():
        mod_transpose(xg[0], 0, 3, 4, HB[0])
        st["ffn_g0_ready"] = True

    def next_slot():
        s = slots[st["slot"] % NSLOT]
        st["slot"] += 1
        return s

    def next_acc():
        banks = st.get("accs") or (PS[0], PS[1], PS[2])
        a = banks[st["acc"] % len(banks)]
        st["acc"] += 1
        return a

    def wload(dst, src):
        op("pool", lambda e: e.dma_start(out=dst, in_=src), reads=[src], writes=[dst], dma=True)

    pf = {}

    def wblock(wd, c0, ncols=512, r0=0, nkc=8, tag=None, issue_only=False):
        if tag is not None and tag in pf and not issue_only:
            return pf.pop(tag)
        s = next_slot()
        if issue_only:
            pf[tag] = s
        wload(s[:, 0:nkc, 0:ncols], wd[r0:r0 + nkc * 128, c0:c0 + ncols].rearrange("(kc p) c -> p kc c", p=128))
        return s

    def ld(dst, src):
        op("sp", lambda e: e.dma_start(out=dst, in_=src), reads=[src], writes=[dst], dma=True)

    def stout(dst, src, is_out):
        op("sp", lambda e: e.dma_start(out=dst, in_=src), reads=[src], writes=[dst], dma=True, is_out=is_out)

    def allgather(src_t, dst_t):
        op("pool", lambda e: e.collective_compute("AllGather", ALU.bypass, replica_groups=GROUPS[0],
                                                  ins=[src_t.ap().opt()], outs=[dst_t.ap().opt()]),
           reads=[src_t.ap()], writes=[dst_t.ap()], dma=True, cc=True)

    hfl = small[:, 58:60]
    oh4 = small[:, 60:64]
    ld(hfl, hflag_d)
    ld(oh4, oh4_d)
    ld(idf[:], ident_d)
    ld(cnds, cnd)
    op("act", lambda e: e.copy(out=idb[:], in_=idf[:]), reads=[idf[:]], writes=[idb[:]])
    op("dve", lambda e: e.memset(onesb[:], 1.0), writes=[onesb[:]])
    op("dve", lambda e: e.memset(w1buf[:], 0.0), writes=[w1buf[:]])
    ohb = small[:, 64:66]
    ld(ohb, ohb_d)
    op("dve", lambda e: e.memset(onesf[:], 1.0), writes=[onesf[:]])
    pt32 = PS[7][:, 0:24].rearrange("p (a b) -> p a b", b=3)
    for kc in range(8):
        op("pe", lambda e, kc=kc: e.transpose(out=pt32[:, kc, :], in_=cnds[:, kc * 128:(kc + 1) * 128], identity=idf[0:3, 0:3]),
           reads=[cnds[:, kc * 128:(kc + 1) * 128], idf[0:3, 0:3]], writes=[pt32[:, kc, :]])
    op("act", lambda e: e.activation(out=silT[:], in_=pt32, func=AF.Silu), reads=[pt32], writes=[silT[:]])

    def modulation_prologue():
        nb_ = 4 * CPC
        ld(badr[0:nb_, :], b_ada_s)
        pm = PS[7][:, 32:32 + nb_ * 3].rearrange("p (a b) -> p a b", b=3)
        for l in range(4):
            c0 = 0
            while c0 < CPC * 128:
                ncols = min(512, CPC * 128 - c0)
                s = next_slot()
                wload(s[:, :, 0:ncols], w_ada_s[l][:, c0:c0 + ncols].rearrange("(kc p) c -> p kc c", p=128))
                for fc in range(ncols // 128):
                    jc = c0 // 128 + fc
                    for kc in range(8):
                        op("pe", lambda e, s=s, fc=fc, kc=kc, l=l, jc=jc: e.matmul(pm[:, l * CPC + jc, :], lhsT=s[:, kc, fc * 128:(fc + 1) * 128],
                                                                                 rhs=silT[:, kc, :], start=(kc == 0), stop=(kc == 7)),
                           reads=[s[:, kc, fc * 128:(fc + 1) * 128], silT[:, kc, :]], writes=[pm[:, l * CPC + jc, :]])
                c0 += ncols
        pb = PS[6][:, 0:nb_]
        op("pe", lambda e: e.transpose(out=pb, in_=badr[0:nb_, :], identity=idf[0:nb_, 0:nb_]),
           reads=[badr[0:nb_, :], idf[0:nb_, 0:nb_]], writes=[pb])
        bT = sgt[0][:, 0:nb_]
        op("act", lambda e: e.copy(out=bT, in_=pb), reads=[pb], writes=[bT])
        op("dve", lambda e: e.memset(modpf[:], 0.0), writes=[modpf[:]])
        op("dve", lambda e: e.tensor_tensor(out=modp, in0=pm, in1=bT.unsqueeze(2).to_broadcast([128, nb_, 3]), op=ALU.add),
           reads=[pm, bT], writes=[modp])
        stout(mod_in.ap(), modpf[:], False)
        for b_ in range(3):
            wblock(conv_w_in[0], b_ * 512, tag=("conv0", b_), issue_only=True)
        allgather(mod_in, mod_out)
        for r_ in range(NC_):
            ld(modA[:, :, r_ * CPC * 3:(r_ + 1) * CPC * 3], mod_out.ap()[r_ * 128:(r_ + 1) * 128, 0:4 * CPC * 3].rearrange("p (l x) -> p l x", l=4))

    def modulation(l):
        mv = modA[:, l, :].rearrange("p (j c) -> p j c", c=3)
        op("dve", lambda e: e.tensor_copy(out=modT[:, :, 0:1], in_=mv[:, :, 0:1]), reads=[mv], writes=[modT[:, :, 0:1]])
        op("dve", lambda e: e.tensor_scalar(out=modT[:, :, 1:2], in0=mv[:, :, 1:2], scalar1=ohb[:, 0:1], scalar2=None, op0=ALU.mult),
           reads=[mv, ohb[:, 0:1]], writes=[modT[:, :, 1:2]])
        op("dve", lambda e: e.scalar_tensor_tensor(out=modT[:, :, 1:2], in0=mv[:, :, 2:3], scalar=ohb[:, 1:2], in1=modT[:, :, 1:2], op0=ALU.mult, op1=ALU.add),
           reads=[mv, ohb[:, 1:2], modT[:, :, 1:2]], writes=[modT[:, :, 1:2]])
        n_ = 0
        for c in range(2):
            for sub, which in ((0, 2), (1, 5)):
                for half in range(2):
                    pg = next_acc()
                    for q in range(4):
                        kc = half * 4 + q
                        d_ = dg[n_ % 2]
                        n_ += 1
                        col = modT[:, which * 8 + kc, c:c + 1]
                        op("dve", lambda e, d_=d_, col=col: e.tensor_scalar(out=d_[:], in0=idf[:], scalar1=col, scalar2=None, op0=ALU.mult),
                           reads=[idf[:], col], writes=[d_[:]])
                        op("pe", lambda e, pg=pg, q=q, d_=d_: e.matmul(pg[:, q * 128:(q + 1) * 128], lhsT=onesf[:], rhs=d_[:], start=True, stop=True),
                           reads=[onesf[:], d_[:]], writes=[pg[:, q * 128:(q + 1) * 128]])
                    dst = gbc[:, c, sub, half * 512:(half + 1) * 512]
                    op("act", lambda e, pg=pg, dst=dst: e.copy(out=dst, in_=pg[:]), reads=[pg[:]], writes=[dst])
        for w in (1, 4):
            v = modT[:, w * 8:(w + 1) * 8, :]
            op("dve", lambda e, v=v: e.tensor_scalar_add(out=v, in0=v, scalar1=1.0), reads=[v], writes=[v])

    def x_src(sidx):
        return xin if sidx == 0 else (XA if sidx % 2 == 1 else XB)

    def x_dst(sidx, last):
        return y if last else (XA if sidx % 2 == 0 else XB)

    loaded = set()

    def load_group(src, g, buf):
        if src is not xin or g in loaded:
            return
        loaded.add(g)
        ld(buf[:], src[g * 512:(g + 1) * 512, :].rearrange("(i p) d -> p i d", p=128))

    def mod_transpose(buf, g, w_sh, w_sc, hT=hT):
        c = 0 if g == 0 else 1
        for i in range(4):
            ptb = PS[5 + (i % 2)][:].bitcast(BF16).rearrange("p (a b) -> p a b", b=128)
            xbt = xb[i % 2]
            op("dve", lambda e, i=i, xbt=xbt: e.tensor_copy(out=xbt[:], in_=buf[:, i, :]), reads=[buf[:, i, :]], writes=[xbt[:]])
            for kc in range(8):
                op("pe", lambda e, kc=kc, xbt=xbt, ptb=ptb: e.transpose(out=ptb[:, kc, :], in_=xbt[:, kc * 128:(kc + 1) * 128], identity=idb[:]),
                   reads=[xbt[:, kc * 128:(kc + 1) * 128], idb[:]], writes=[ptb[:, kc, :]])
            for kc in range(8):
                dst = hT[:, kc, i * 128:(i + 1) * 128]
                shc = modT[:, w_sh * 8 + kc, c:c + 1]
                scc = modT[:, w_sc * 8 + kc, c:c + 1]
                if True:
                    op("act", lambda e, kc=kc, dst=dst, ptb=ptb, shc=shc, scc=scc: e.activation(out=dst, in_=ptb[:, kc, :], func=AF.Identity, bias=shc, scale=scc),
                       reads=[ptb[:, kc, :], shc, scc], writes=[dst])
                else:
                    op("dve", lambda e, kc=kc, dst=dst, ptb=ptb, shc=shc, scc=scc: e.tensor_scalar(out=dst, in0=ptb[:, kc, :], scalar1=scc, scalar2=shc, op0=ALU.mult, op1=ALU.add),
                       reads=[ptb[:, kc, :], shc, scc], writes=[dst])

    def fm_matmul(s, fc, rhs_fn, n, nkc=8):
        a = next_acc()
        for kc in range(nkc):
            r = rhs_fn(kc)
            op("pe", lambda e, a=a, s=s, fc=fc, kc=kc, r=r: e.matmul(a[:, 0:n], lhsT=s[:, kc, fc * 128:(fc + 1) * 128], rhs=r,
                                                                     start=(kc == 0), stop=(kc == nkc - 1)),
               reads=[s[:, kc, fc * 128:(fc + 1) * 128], r], writes=[a[:, 0:n]])
        return a

    def load_ln(l, which):
        ld(lnbc[:, 0, :], ln_g[l, which].partition_broadcast(128))
        ld(lnbc[:, 1, :], ln_b[l, which].partition_broadcast(128))

    def out_proj_epilogue(wd, nkc, lhsT_fn, buf, g, sub, dst, is_out):
        out_proj_multi(wd, nkc, [(g, buf, lhsT_fn)], sub, dst, is_out)

    def out_proj_multi(wd, nkc, items, sub, dst, is_out):
        nb = (nkc + 7) // 8
        blocks = []
        for half in range(2):
            bl = []
            for b in range(nb):
                kk = min(8, nkc - b * 8)
                bl.append(wblock(wd, half * 512, 512, r0=b * 1024, nkc=kk))
            blocks.append(bl)
        cnt = 0
        for (g, buf, lhsT_fn) in items:
            c = 0 if g == 0 else 1
            for half in range(2):
                for i in range(4):
                    pz = PS[3 + (cnt % 2)]
                    tt = ttmp[cnt % 2]
                    cnt += 1
                    for kc in range(nkc):
                        s = blocks[half][kc // 8]
                        lt = lhsT_fn(kc, i)
                        op("pe", lambda e, pz=pz, s=s, kc=kc, lt=lt: e.matmul(pz[:], lhsT=lt, rhs=s[:, kc % 8, :], start=(kc == 0), stop=(kc == nkc - 1)),
                           reads=[lt, s[:, kc % 8, :]], writes=[pz[:]])
                    xs = buf[:, i, half * 512:(half + 1) * 512]
                    gv = gbc[:, c, sub, half * 512:(half + 1) * 512]
                    op("dve", lambda e, pz=pz, tt=tt, gv=gv: e.tensor_tensor(out=tt[:], in0=pz[:], in1=gv, op=ALU.mult),
                       reads=[pz[:], gv], writes=[tt[:]])
                    op("dve", lambda e, xs=xs, tt=tt: e.scalar_tensor_tensor(out=xs, in0=xs, scalar=ALPHA, in1=tt[:], op0=ALU.mult, op1=ALU.add),
                       reads=[xs, tt[:]], writes=[xs])
            T4 = range(4)
            sbs = [lnst[:, 16 * i:16 * i + 16] for i in T4]
            for i in T4:
                for h in range(2):
                    op("dve", lambda e, i=i, h=h, sb=sbs[i], buf=buf: e.bn_stats(out=sb[:, h * 6:(h + 1) * 6], in_=buf[:, i, h * 512:(h + 1) * 512]),
                       reads=[buf[:, i, h * 512:(h + 1) * 512]], writes=[sbs[i][:, h * 6:(h + 1) * 6]])
            for i in T4:
                op("dve", lambda e, sb=sbs[i]: e.bn_aggr(out=sb[:, 12:14], in_=sb[:, 0:12].rearrange("p (a b) -> p a b", b=6)), reads=[sbs[i][:, 0:12]], writes=[sbs[i][:, 12:14]])
            for i in T4:
                op("act", lambda e, sb=sbs[i]: e.activation(out=sb[:, 14:15], in_=sb[:, 13:14], func=AF.Sqrt, bias=LN_EPS, scale=1.0),
                   reads=[sbs[i][:, 13:14]], writes=[sbs[i][:, 14:15]])
            for i in T4:
                op("dve", lambda e, sb=sbs[i]: e.reciprocal(out=sb[:, 14:15], in_=sb[:, 14:15]), reads=[sbs[i][:, 14:15]], writes=[sbs[i][:, 14:15]])
            for i in T4:
                op("dve", lambda e, sb=sbs[i]: e.tensor_scalar(out=sb[:, 15:16], in0=sb[:, 12:13], scalar1=sb[:, 14:15], scalar2=-1.0, op0=ALU.mult, op1=ALU.mult),
                   reads=[sbs[i][:, 12:13], sbs[i][:, 14:15]], writes=[sbs[i][:, 15:16]])
            for i in T4:
                z = buf[:, i, :]
                op("act", lambda e, z=z, sb=sbs[i]: e.activation(out=z, in_=z, func=AF.Identity, bias=sb[:, 15:16], scale=sb[:, 14:15]),
                   reads=[z, sbs[i][:, 14:16]], writes=[z])
            for i in T4:
                z = buf[:, i, :]
                op("dve", lambda e, z=z: e.tensor_tensor(out=z, in0=z, in1=lnbc[:, 0, :], op=ALU.mult), reads=[z, lnbc[:, 0, :]], writes=[z])
            for i in T4:
                z = buf[:, i, :]
                op("dve", lambda e, z=z: e.tensor_tensor(out=z, in0=z, in1=lnbc[:, 1, :], op=ALU.add), reads=[z, lnbc[:, 1, :]], writes=[z])
            if is_out:
                stout(dst[g * 512:(g + 1) * 512, :].rearrange("(i p) d -> p i d", p=128), buf[:], True)

    def ffn_sublayer(l, sidx, last):
        src, dst = x_src(sidx), x_dst(sidx, last)
        load_ln(l, 1)
        acts = [carve(0, [128, 22, 512], BF16), carve(22528, [128, 22, 512], BF16)]
        hTs = HB
        st["accs"] = (PS[0], PS[1], PS[2], PS[7])
        for g in range(NG):
            load_group(src, g, xg[g])
        if not st["ffn_g0_ready"]:
            mod_transpose(xg[0], 0, 3, 4, hTs[0])
        st["ffn_g0_ready"] = False
        wd = ffn_w_in[l]
        n_ = 0
        for j in range(11):
            s = next_slot()
            wload(s[:, :, 0:256], wd[:, j * 256:(j + 1) * 256].rearrange("(kc p) c -> p kc c", p=128))
            wload(s[:, :, 256:512], wd[:, DFF + j * 256:DFF + (j + 1) * 256].rearrange("(kc p) c -> p kc c", p=128))
            order = [(q, g) for q in range(2) for g in range(NG)] if j > 0 else [(0, 0), (1, 0), (0, 1), (1, 1)]
            for (q, g) in order:
                if j == 0 and (q, g) == (0, 1):
                    mod_transpose(xg[1], 1, 3, 4, hTs[1])
                if True:
                    hTg = hTs[g]
                    a = fm_matmul(s, q, lambda kc, hTg=hTg: hTg[:, kc, 0:512], 512)
                    sg = sgt[n_ % 2]
                    n_ += 1
                    op("act", lambda e, a=a, sg=sg: e.activation(out=sg[:], in_=a[:], func=AF.Silu), reads=[a[:]], writes=[sg[:]])
                    a2 = fm_matmul(s, 2 + q, lambda kc, hTg=hTg: hTg[:, kc, 0:512], 512)
                    dsta = acts[g][:, 2 * j + q, :]
                    op("dve", lambda e, a2=a2, sg=sg, dsta=dsta: e.tensor_tensor(out=dsta, in0=a2[:], in1=sg[:], op=ALU.mult),
                       reads=[a2[:], sg[:]], writes=[dsta])
        items = []
        for g in range(NG):
            actg = acts[g]
            items.append((g, xg[g], (lambda kc, i, actg=actg: actg[:, kc, i * 128:(i + 1) * 128])))
        st["accs"] = None
        out_proj_multi(ffn_w_out[l], 22, items, 1, dst, last)

    def conv_sublayer(l, j, sidx, last=False):
        src, dst = x_src(sidx), x_dst(sidx, last)
        load_ln(l, 0)
        ld(cwr[:], conv_w[j].rearrange("k (kc p) -> (k kc) p", p=128))
        pc = PS[7][:, 200:224]
        op("pe", lambda e: e.transpose(out=pc, in_=cwr[:], identity=idf[0:24, 0:24]), reads=[cwr[:], idf[0:24, 0:24]], writes=[pc])
        op("act", lambda e: e.copy(out=cwT[:], in_=pc), reads=[pc], writes=[cwT[:]])
        bgs = carve(0, [128, 8, 512], BF16)
        cgs = carve(8192, [128, 8, 512], F32)
        uu = carve(24576, [128, 8, 516], F32)
        cact = carve(41472, [128, 8, 512], BF16)
        cgh = carve(49664, [128, 8, 2], F32)
        wd = conv_w_in[j]
        load_group(src, 0, xg[0])
        load_group(src, 1, xg[1])
        if src is xin:
            stout(bnd_in.ap()[0:1, :], src[512:513, :], False)
            stout(bnd_in.ap()[1:2, :], src[1023:1024, :], False)
        else:
            stout(bnd_in.ap()[0:1, :], xg[1][0:1, 0, :], False)
            stout(bnd_in.ap()[1:2, :], xg[1][127:128, 3, :], False)
        allgather(bnd_in, bnd_out)
        ld(misc1[32:40, :], bnd_out.ap())
        ld(misc2[32:40, 0:2], sel_d)
        for hf_ in range(2):
            a_ = next_acc()
            op("pe", lambda e, a_=a_, hf_=hf_: e.matmul(a_[0:2, :], lhsT=misc2[32:40, 0:2], rhs=misc1[32:40, hf_ * 512:(hf_ + 1) * 512], start=True, stop=True),
               reads=[misc2[32:40, 0:2], misc1[32:40, hf_ * 512:(hf_ + 1) * 512]], writes=[a_[0:2, :]])
            op("act", lambda e, a_=a_, hf_=hf_: e.copy(out=misc1[0:2, hf_ * 512:(hf_ + 1) * 512], in_=a_[0:2, :]),
               reads=[a_[0:2, :]], writes=[misc1[0:2, hf_ * 512:(hf_ + 1) * 512]])
        for g in range(NG):
            buf = xg[g % 2]
            if g + 1 < NG:
                load_group(src, g + 1, xg[(g + 1) % 2])
            hg = HB[g]
            if g == 0:
                mod_transpose(buf, g, 0, 1, hg)
            has_prev = g >= 1
            has_next = g >= 1
            if g == 0:
                segs = [(0, 256, 1), (256, 256, 259)]
                zero_cols = [0, 257, 258, 515]
            else:
                segs = [(0, 512, 1)]
                zero_cols = ([] if has_prev else [0]) + ([] if has_next else [513])
            for zc in zero_cols:
                op("dve", lambda e, zc=zc: e.memset(uu[:, :, zc:zc + 1], 0.0), writes=[uu[:, :, zc:zc + 1]])
            nh = 0
            if has_prev or has_next:
                nh = 2
                op("dve", lambda e: e.tensor_copy(out=xhb, in_=xh), reads=[xh], writes=[xhb])
                pth = PS[5][:].bitcast(BF16)[:, 0:16].rearrange("p (a b) -> p a b", b=2)
                for kc in range(8):
                    op("pe", lambda e, kc=kc: e.transpose(out=pth[:, kc, :], in_=xhb[:, kc * 128:(kc + 1) * 128], identity=idb[0:2, 0:2]),
                       reads=[xhb[:, kc * 128:(kc + 1) * 128], idb[0:2, 0:2]], writes=[pth[:, kc, :]])
                for kc in range(8):
                    dsth = hT[:, kc, 512:514]
                    op("act", lambda e, kc=kc, dsth=dsth: e.activation(out=dsth, in_=pth[:, kc, :], func=AF.Identity,
                                                                       bias=modT[:, 0 * 8 + kc, 1:2], scale=modT[:, 1 * 8 + kc, 1:2]),
                       reads=[pth[:, kc, :], modT[:, kc, 1:2], modT[:, 8 + kc, 1:2]], writes=[dsth])
            ph = PS[7][:, 256:288].rearrange("p (a b) -> p a b", b=2)
            for blk in range(6):
                s = wblock(wd, blk * 512, tag=("conv%d" % l, blk) if g == 0 else None)
                for fc in range(4):
                    ch = (blk % 2) * 4 + fc
                    a = fm_matmul(s, fc, lambda kc, hg=hg: hg[:, kc, 0:512], 512)
                    if blk < 2:
                        op("act", lambda e, a=a, ch=ch: e.copy(out=bgs[:, ch, :], in_=a[:]), reads=[a[:]], writes=[bgs[:, ch, :]])
                    elif blk < 4:
                        op("act", lambda e, a=a, ch=ch: e.copy(out=cgs[:, ch, :], in_=a[:]), reads=[a[:]], writes=[cgs[:, ch, :]])
                    else:
                        for (c0, n, d0) in segs:
                            op("dve", lambda e, a=a, ch=ch, c0=c0, n=n, d0=d0: e.tensor_tensor(out=uu[:, ch, d0:d0 + n], in0=a[:, c0:c0 + n],
                                                                                               in1=cgs[:, ch, c0:c0 + n], op=ALU.mult),
                               reads=[a[:, c0:c0 + n], cgs[:, ch, c0:c0 + n]], writes=[uu[:, ch, d0:d0 + n]])
                    if nh and blk >= 2:
                        hidx = (blk - 2) * 4 + fc
                        for kc in range(8):
                            op("pe", lambda e, s=s, fc=fc, kc=kc, hidx=hidx: e.matmul(ph[:, hidx, :], lhsT=s[:, kc, fc * 128:(fc + 1) * 128],
                                                                                      rhs=hT[:, kc, 512:514], start=(kc == 0), stop=(kc == 7)),
                               reads=[s[:, kc, fc * 128:(fc + 1) * 128], hT[:, kc, 512:514]], writes=[ph[:, hidx, :]])
            if g == 1 and not last:
                pre_ffn()
            if nh:
                op("act", lambda e: e.copy(out=cgh[:], in_=ph[:, 0:8, :]), reads=[ph[:, 0:8, :]], writes=[cgh[:]])
                if has_prev:
                    op("dve", lambda e: e.tensor_tensor(out=uu[:, :, 0:1], in0=ph[:, 8:16, 0:1], in1=cgh[:, :, 0:1], op=ALU.mult),
                       reads=[ph[:, 8:16, 0:1], cgh[:, :, 0:1]], writes=[uu[:, :, 0:1]])
                if has_next:
                    op("dve", lambda e: e.tensor_tensor(out=uu[:, :, 513:514], in0=ph[:, 8:16, 1:2], in1=cgh[:, :, 1:2], op=ALU.mult),
                       reads=[ph[:, 8:16, 1:2], cgh[:, :, 1:2]], writes=[uu[:, :, 513:514]])
                op("dve", lambda e: e.tensor_scalar(out=uu[:, :, 0:1], in0=uu[:, :, 0:1], scalar1=hfl[:, 0:1], scalar2=None, op0=ALU.mult),
                   reads=[uu[:, :, 0:1], hfl[:, 0:1]], writes=[uu[:, :, 0:1]])
                op("dve", lambda e: e.tensor_scalar(out=uu[:, :, 513:514], in0=uu[:, :, 513:514], scalar1=hfl[:, 1:2], scalar2=None, op0=ALU.mult),
                   reads=[uu[:, :, 513:514], hfl[:, 1:2]], writes=[uu[:, :, 513:514]])
            items_ = [(ch, c0, n, d0, cgs[:, ch, c0:c0 + n]) for ch in range(8) for (c0, n, d0) in segs]
            for (ch, c0, n, d0, yv) in items_:
                op("dve", lambda e, ch=ch, n=n, d0=d0, yv=yv: e.tensor_scalar(out=yv, in0=uu[:, ch, d0:d0 + n], scalar1=cwT[:, 8 + ch:9 + ch], scalar2=None, op0=ALU.mult),
                   reads=[uu[:, ch, d0:d0 + n], cwT[:, 8 + ch:9 + ch]], writes=[yv])
            for (ch, c0, n, d0, yv) in items_:
                op("dve", lambda e, ch=ch, n=n, d0=d0, yv=yv: e.scalar_tensor_tensor(out=yv, in0=uu[:, ch, d0 - 1:d0 - 1 + n], scalar=cwT[:, ch:ch + 1], in1=yv, op0=ALU.mult, op1=ALU.add),
                   reads=[uu[:, ch, d0 - 1:d0 - 1 + n], cwT[:, ch:ch + 1], yv], writes=[yv])
            for (ch, c0, n, d0, yv) in items_:
                op("dve", lambda e, ch=ch, n=n, d0=d0, yv=yv: e.scalar_tensor_tensor(out=yv, in0=uu[:, ch, d0 + 1:d0 + 1 + n], scalar=cwT[:, 16 + ch:17 + ch], in1=yv, op0=ALU.mult, op1=ALU.add),
                   reads=[uu[:, ch, d0 + 1:d0 + 1 + n], cwT[:, 16 + ch:17 + ch], yv], writes=[yv])
            for (ch, c0, n, d0, yv) in items_:
                op("dve", lambda e, ch=ch, n=n, c0=c0, yv=yv: e.tensor_tensor(out=cact[:, ch, c0:c0 + n], in0=yv, in1=bgs[:, ch, c0:c0 + n], op=ALU.mult),
                   reads=[yv, bgs[:, ch, c0:c0 + n]], writes=[cact[:, ch, c0:c0 + n]])
            if g == 0:
                mod_transpose(xg[1], 1, 0, 1, HB[1])
            out_proj_epilogue(conv_w_out[j], 8, lambda kc, i: cact[:, kc, i * 128:(i + 1) * 128], buf, g, 0, dst, last)

    def attn_sublayer(l, j, sidx, last=False):
        src, dst = x_src(sidx), x_dst(sidx, last)
        load_ln(l, 0)
        qg = carve(0, [128, 128], F32)
        kg = carve(512, [128, 128], F32)
        KT = carve(1024, [128, 2, 2304], BF16)
        VA = carve(10240, [128, 18, 256], BF16)
        KTp = carve(19456, [128, 2, 512], BF16)
        Vp = carve(21504, [128, 4, 256], BF16)
        qT = carve(23552, [128, 8, 512], BF16)
        oT = carve(31744, [128, 8, 512], BF16)
        sq = carve(39936, [128, 1280], F32)
        qn = carve(45056, [128, 1280], F32)
        rt = carve(50176, [128, 2, 640], F32)
        qb = carve(55296, [128, 1280], BF16)
        cs = carve(57856, [128, 2, 64], F32)
        kvo = carve(58368, [128, 2, 256], F32)
        rden = carve(60416, [128, 512], F32)
        pTb = [carve(62464, [128, 512], BF16), carve(63488, [128, 512], BF16)]
        ss = small[:, 32:44]
        rs = small[:, 44:56]
        ld(qg, attn_q_norm[j].partition_broadcast(128))
        ld(kg, attn_k_norm[j].partition_broadcast(128))
        wqkv = attn_w_qkv[j]
        SCALE = 128.0 ** -0.5

        cs_all = carve(58368, [128, 4, 2, 64], F32)

        def load_rope():
            ld(cs_all[:, :, 0, :], rcos.rearrange("(t p) f -> p t f", p=128))
            ld(cs_all[:, :, 1, :], rsin.rearrange("(t p) f -> p t f", p=128))

        load_rope()

        def norm_rope_jobs(jobs):
            for J in jobs:
                H, c0 = J["H"], J["c0"]
                J["sqv"] = sq[:, c0:c0 + H * 128]
                J["qnv"] = qn[:, c0:c0 + H * 128]
                J["ssv"] = ss[:, J["so"]:J["so"] + H]
                J["rsv"] = rs[:, J["so"]:J["so"] + H]
            for J in jobs:
                op("act", lambda e, J=J: e.activation(out=J["sqv"], in_=J["pk"], func=AF.Square), reads=[J["pk"]], writes=[J["sqv"]])
            for J in jobs:
                op("dve", lambda e, J=J: e.tensor_reduce(out=J["ssv"], in_=J["sqv"].rearrange("p (h d) -> p h d", d=128), axis=AX.X, op=ALU.add),
                   reads=[J["sqv"]], writes=[J["ssv"]])
            for J in jobs:
                op("act", lambda e, J=J: e.activation(out=J["rsv"], in_=J["ssv"], func=AF.Sqrt, bias=RMS_EPS, scale=1.0 / 128.0), reads=[J["ssv"]], writes=[J["rsv"]])
            for J in jobs:
                op("dve", lambda e, J=J: e.reciprocal(out=J["rsv"], in_=J["rsv"]), reads=[J["rsv"]], writes=[J["rsv"]])
            for J in jobs:
                H = J["H"]
                q3 = J["qnv"].rearrange("p (h d) -> p h d", d=128)
                op("dve", lambda e, J=J, q3=q3, H=H: e.tensor_tensor(out=q3, in0=J["pk"].rearrange("p (h d) -> p h d", d=128),
                                                                 in1=J["rsv"].unsqueeze(2).to_broadcast([128, H, 128]), op=ALU.mult),
                   reads=[J["pk"], J["rsv"]], writes=[J["qnv"]])
            for J in jobs:
                H = J["H"]
                q3 = J["qnv"].rearrange("p (h d) -> p h d", d=128)
                op("dve", lambda e, J=J, q3=q3, H=H: e.tensor_tensor(out=q3, in0=q3, in1=J["gain"].unsqueeze(1).to_broadcast([128, H, 128]), op=ALU.mult),
                   reads=[J["qnv"], J["gain"]], writes=[J["qnv"]])
            ropes = []
            for J in jobs:
                H = J["H"]
                if J["ti"] is None:
                    op("act", lambda e, J=J: e.copy(out=J["out_bf"], in_=J["qnv"]), reads=[J["qnv"]], writes=[J["out_bf"]])
                    continue
                qv = J["qnv"].rearrange("p (h a t f) -> p h a t f", h=H, a=2, t=2, f=32)
                ob = J["out_bf"].rearrange("p (h a t f) -> p h a t f", h=H, a=2, t=2, f=32)
                cst = cs_all[:, J["ti"], :, :]
                R = dict(J=J, x1=qv[:, :, :, 0, :], x2=qv[:, :, :, 1, :], o1=ob[:, :, :, 0, :], o2=ob[:, :, :, 1, :],
                         cosb=cst[:, 0, :].rearrange("p (a f) -> p a f", a=2).unsqueeze(1).to_broadcast([128, H, 2, 32]),
                         sinb=cst[:, 1, :].rearrange("p (a f) -> p a f", a=2).unsqueeze(1).to_broadcast([128, H, 2, 32]),
                         ta=rt[:, 0, J["ro"]:J["ro"] + H * 64].rearrange("p (h a f) -> p h a f", h=H, a=2, f=32),
                         tb=rt[:, 1, J["ro"]:J["ro"] + H * 64].rearrange("p (h a f) -> p h a f", h=H, a=2, f=32),
                         rd=[J["qnv"], cst])
                ropes.append(R)
            for (dst_, a_, b_, opx) in (("ta", "x1", "cosb", ALU.mult), ("tb", "x2", "sinb", ALU.mult), ("o1", "ta", "tb", ALU.subtract),
                                        ("ta", "x2", "cosb", ALU.mult), ("tb", "x1", "sinb", ALU.mult), ("o2", "ta", "tb", ALU.add)):
                for R in ropes:
                    wr = [R["J"]["out_bf"]] if dst_ in ("o1", "o2") else [R[dst_]]
                    rd = [R["ta"], R["tb"]] if dst_ in ("o1", "o2") else R["rd"]
                    op("dve", lambda e, R=R, dst_=dst_, a_=a_, b_=b_, opx=opx: e.tensor_tensor(out=R[dst_], in0=R[a_], in1=R[b_], op=opx), reads=rd, writes=wr)

        def proj_tile(s, i, ncols=512):
            a = next_acc()
            hh = cur["h"]
            for kc in range(8):
                op("pe", lambda e, a=a, s=s, i=i, kc=kc, hh=hh: e.matmul(a[:, 0:ncols], lhsT=hh[:, kc, i * 128:(i + 1) * 128], rhs=s[:, kc, 0:ncols], start=(kc == 0), stop=(kc == 7)),
                   reads=[hh[:, kc, i * 128:(i + 1) * 128], s[:, kc, 0:ncols]], writes=[a[:, 0:ncols]])
            return a

        def transposes(srcb, nh, dstv, par):
            ptb = PS[5 + (par % 2)][:].bitcast(BF16).rearrange("p (a b) -> p a b", b=128)
            for h in range(nh):
                op("pe", lambda e, h=h, ptb=ptb: e.transpose(out=ptb[:, h, :], in_=srcb[:, h * 128:(h + 1) * 128], identity=idb[:]),
                   reads=[srcb[:, h * 128:(h + 1) * 128], idb[:]], writes=[ptb[:, h, :]])
            op("act", lambda e, ptb=ptb: e.copy(out=dstv, in_=ptb[:, 0:nh, :]), reads=[ptb[:, 0:nh, :]], writes=[dstv])

        mod_transpose(xg[0], 0, 0, 1, HB[0])
        ctmp = qb[:, 0:512].rearrange("p (t c) -> p t c", c=256)
        wload(ctmp, ck.rearrange("(t p) c -> p t c", p=128))
        wload(VA[:, 0:2, :], cv.rearrange("(t p) c -> p t c", p=128))
        for t in range(2):
            transposes(ctmp[:, t, :], 2, KT[:, :, t * 128:(t + 1) * 128], t)
        load_group(src, 1, xg[1])
        for g in range(1, NG):
            buf = xg[g % 2]
            if g + 1 < NG:
                load_group(src, g + 1, xg[(g + 1) % 2])
            mod_transpose(buf, g, 0, 1, HB[1])
            cur["h"] = HB[1]
            s2 = wblock(wqkv, 1024)
            for ip in (0, 2):
                pk2 = [proj_tile(s2, ip), proj_tile(s2, ip + 1)]
                jobs = [dict(pk=pk2[u][:, 0:256], H=2, gain=kg, ti=ip + u, out_bf=qb[:, u * 256:(u + 1) * 256], c0=u * 256, so=2 * u, ro=128 * u) for u in range(2)]
                norm_rope_jobs(jobs)
                for u in range(2):
                    ti = ip + u
                    transposes(qb[:, u * 256:(u + 1) * 256], 2, KT[:, :, 256 + ti * 128:256 + (ti + 1) * 128], u)
                    op("act", lambda e, pkv=pk2[u], ti=ti: e.copy(out=VA[:, 2 + ti, :], in_=pkv[:, 256:512]), reads=[pk2[u][:, 256:512]], writes=[VA[:, 2 + ti, :]])

        stout(kv_in.ap()[:, 0:1024].rearrange("p (a t) -> p a t", a=2), KT[:, :, 256:768], False)
        stout(kv_in.ap()[:, 1024:2048].rearrange("p (a c) -> p a c", a=4), VA[:, 2:6, :], False)
        for t_, c_ in (("aq0", 0), ("aq1", 512), ("akv", 1024)):
            wblock(wqkv, c_, tag=t_, issue_only=True)
        allgather(kv_in, kv_out)

        def load_gathered_kv():
            for r_ in range(4):
                ld(KT[:, :, 256 + r_ * 512:256 + (r_ + 1) * 512], kv_out.ap()[r_ * 128:(r_ + 1) * 128, 0:1024].rearrange("p (a t) -> p a t", a=2))
                ld(VA[:, 2 + r_ * 4:6 + r_ * 4, :], kv_out.ap()[r_ * 128:(r_ + 1) * 128, 1024:2048].rearrange("p (a c) -> p a c", a=4))

        def attend(n0, n, KTb, Vb, ktiles):
            nk_ = len(ktiles)
            for h in range(8):
                kv = h // 4
                po, pd = (PS[3], PS[4]) if h % 2 == 0 else (PS[6], PS[7])
                pTs = {}

                def score(idx):
                    kt = ktiles[idx]
                    ps_ = next_acc()
                    op("pe", lambda e, ps_=ps_, kt=kt, kv=kv, h=h: e.matmul(ps_[:, 0:n], lhsT=KTb[:, kv, kt * 128:(kt + 1) * 128], rhs=qT[:, h, n0:n0 + n], start=True, stop=True),
                       reads=[KTb[:, kv, kt * 128:(kt + 1) * 128], qT[:, h, n0:n0 + n]], writes=[ps_[:, 0:n]])
                    pT = pTb[idx % 2]
                    op("act", lambda e, ps_=ps_, pT=pT: e.activation(out=pT[:, 0:n], in_=ps_[:, 0:n], func=AF.Exp, scale=SCALE), reads=[ps_[:, 0:n]], writes=[pT[:, 0:n]])
                    pTs[idx] = pT

                def pv(idx):
                    kt = ktiles[idx]
                    pT = pTs[idx]
                    op("pe", lambda e, kt=kt, pT=pT, idx=idx, po=po, kv=kv: e.matmul(po[:, 0:n], lhsT=Vb[:, kt, kv * 128:(kv + 1) * 128], rhs=pT[:, 0:n], start=(idx == 0), stop=(idx == nk_ - 1)),
                       reads=[Vb[:, kt, kv * 128:(kv + 1) * 128], pT[:, 0:n]], writes=[po[:, 0:n]])
                    op("pe", lambda e, pT=pT, idx=idx, pd=pd: e.matmul(pd[:, 0:n], lhsT=onesb[:], rhs=pT[:, 0:n], start=(idx == 0), stop=(idx == nk_ - 1)),
                       reads=[onesb[:], pT[:, 0:n]], writes=[pd[:, 0:n]])

                score(0)
                for idx in range(nk_):
                    if idx + 1 < nk_:
                        score(idx + 1)
                    pv(idx)
                op("dve", lambda e, pd=pd: e.reciprocal(out=rden[:, 0:n], in_=pd[:, 0:n]), reads=[pd[:, 0:n]], writes=[rden[:, 0:n]])
                op("dve", lambda e, po=po, h=h: e.tensor_tensor(out=oT[:, h, n0:n0 + n], in0=po[:, 0:n], in1=rden[:, 0:n], op=ALU.mult),
                   reads=[po[:, 0:n], rden[:, 0:n]], writes=[oT[:, h, n0:n0 + n]])

        load_group(src, 0, xg[0])
        for g in range(NG):
            buf = xg[g % 2]
            if g + 1 < NG:
                load_group(src, g + 1, xg[(g + 1) % 2])
            if g == 1:
                load_gathered_kv()
                load_rope()
            cur["h"] = HB[g]
            s0 = wblock(wqkv, 0, tag="aq0")
            s1 = wblock(wqkv, 512, tag="aq1")
            s2 = wblock(wqkv, 1024, tag="akv") if g == 0 else None
            for i in range(4):
                ti = None if g == 0 else i
                pqs = [proj_tile(s0, i), proj_tile(s1, i)]
                jobs = [dict(pk=pqs[hf][:], H=4, gain=qg, ti=ti, out_bf=qb[:, hf * 512:(hf + 1) * 512], c0=hf * 512, so=4 * hf, ro=256 * hf) for hf in range(2)]
                if g == 0:
                    pkv = proj_tile(s2, i)
                    jobs.append(dict(pk=pkv[:, 0:256], H=2, gain=kg, ti=None, out_bf=qb[:, 1024:1280], c0=1024, so=8, ro=512))
                norm_rope_jobs(jobs)
                transposes(qb[:, 0:1024], 8, qT[:, :, i * 128:(i + 1) * 128], i)
                if g == 0:
                    transposes(qb[:, 1024:1280], 2, KTp[:, :, i * 128:(i + 1) * 128], i + 1)
                    op("act", lambda e, i=i: e.copy(out=kvo[:, 0, :], in_=qn[:, 1024:1280]), reads=[qn[:, 1024:1280]], writes=[kvo[:, 0, :]])
                    op("act", lambda e, pkv=pkv: e.copy(out=kvo[:, 1, :], in_=pkv[:, 256:512]), reads=[pkv[:, 256:512]], writes=[kvo[:, 1, :]])
                    op("act", lambda e, i=i: e.copy(out=Vp[:, i, :], in_=kvo[:, 1, :]), reads=[kvo[:, 1, :]], writes=[Vp[:, i, :]])
                    stout(nk[i * 128:(i + 1) * 128, :], kvo[:, 0, :], True)
                    stout(nv[i * 128:(i + 1) * 128, :], kvo[:, 1, :], True)
            if g == 0:
                for sidx_ in range(2):
                    attend(sidx_ * 256, 256, KTp, Vp, [2 * sidx_, 2 * sidx_ + 1])
            else:
                if not last:
                    pre_ffn()
                attend(0, 512, KT, VA, list(range(18)))
            out_proj_epilogue(attn_w_o[j], 8, lambda kc, i: oT[:, kc, i * 128:(i + 1) * 128], buf, g, 0, dst, last)

    def gla_sublayer(l, j, sidx, last=False):
        src, dst = x_src(sidx), x_dst(sidx, last)
        load_ln(l, 0)
        G = carve(0, [128, 4, 2, 512], F32)
        vt = carve(0, [128, 4, 1024], BF16)
        ogT = carve(8192, [128, 8, 512], BF16)
        Etok = carve(16384, [128, 4, 2, 512], BF16)
        EposT = carve(24576, [128, 2, 4, 512], BF16)
        oT = carve(16384, [128, 8, 512], F32)
        EnegT = carve(32768, [128, 2, 4, 512], BF16)
        ktok = carve(32768, [128, 4, 2, 512], BF16)
        qtil = carve(40960, [128, 2, 4, 512], BF16)
        oact = carve(40960, [128, 8, 512], BF16)
        ktilT = carve(49152, [128, 2, 4, 512], BF16)
        sqt = carve(49152, [128, 2, 512], BF16)
        S = carve(57344, [128, 4, 256], F32)
        Sbf = carve(61440, [128, 4, 256], BF16)
        ebc = carve(63488, [128, 2, 4, 8], F32)
        pdp = carve(63744, [128, 2, 4], F32)
        Am = [carve(63808, [128, 128], BF16), carve(64064, [128, 128], BF16)]
        tri = [misc1[:, 0:128], misc1[:, 128:256]]
        ld(tri[0], trif_d)
        ld(tri[1], trib_d)
        gnr = misc2[0:2, 0:128]
        ld(gnr, gla_norm[j].rearrange("(a p) -> a p", p=128))
        pgn = PS[7][:, 300:302]
        op("pe", lambda e: e.transpose(out=pgn, in_=gnr, identity=idf[0:2, 0:2]), reads=[gnr, idf[0:2, 0:2]], writes=[pgn])
        gnT = small[:, 56:58]
        op("act", lambda e: e.copy(out=gnT, in_=pgn), reads=[pgn], writes=[gnT])
        rTs = ttmp[0][0:64, 0:256].bitcast(BF16)
        w2s = sgt[0][0:64, 0:256].bitcast(BF16)
        bgb = [sgt[1], ttmp[1]]
        w = gla_w_in[j]
        for d in range(2):
            wload(w2s[d * 32:d * 32 + 16, :], gla_w_gate2[j, d])
            wload(w1buf[:, :, d * 32:d * 32 + 16], gla_w_gate1[j, d].rearrange("(kc p) r -> p kc r", p=128))

        def tile_proj(s, i, c0=0, ncols=512):
            a = next_acc()
            hh = cur["h"]
            for kc in range(8):
                op("pe", lambda e, a=a, s=s, i=i, kc=kc, hh=hh: e.matmul(a[:, 0:ncols], lhsT=hh[:, kc, i * 128:(i + 1) * 128], rhs=s[:, kc, c0:c0 + ncols], start=(kc == 0), stop=(kc == 7)),
                   reads=[hh[:, kc, i * 128:(i + 1) * 128], s[:, kc, c0:c0 + ncols]], writes=[a[:, 0:ncols]])
            return a

        def gates_and_decays(full):
            sg1 = w1buf
            for d in range(2):
                ld(bgb[d][:], gla_b_gate[j, d].partition_broadcast(128))
            pr = next_acc()
            hh = cur["h"]
            for kc in range(8):
                op("pe", lambda e, kc=kc, hh=hh: e.matmul(pr[0:64, :], lhsT=sg1[:, kc, 0:64], rhs=hh[:, kc, 0:512], start=(kc == 0), stop=(kc == 7)),
                   reads=[sg1[:, kc, 0:64], hh[:, kc, 0:512]], writes=[pr[0:64, :]])
            op("act", lambda e: e.copy(out=rTs, in_=pr[0:64, :]), reads=[pr[0:64, :]], writes=[rTs])
            for i in range(4):
                for d in range(2):
                    pz = next_acc()
                    op("pe", lambda e, pz=pz, i=i, d=d: e.matmul(pz[:], lhsT=rTs[d * 32:d * 32 + 16, i * 128:(i + 1) * 128], rhs=w2s[d * 32:d * 32 + 16, :], start=True, stop=True),
                       reads=[rTs[d * 32:d * 32 + 16, i * 128:(i + 1) * 128], w2s[d * 32:d * 32 + 16, :]], writes=[pz[:]])
                    gv = G[:, i, d, :]
                    op("dve", lambda e, pz=pz, gv=gv, d=d: e.tensor_tensor(out=gv, in0=pz[:], in1=bgb[d][:], op=ALU.add), reads=[pz[:], bgb[d][:]], writes=[gv])
            Gf = carve(0, [128, 4096], F32)
            op("act", lambda e: e.activation(out=Gf, in_=Gf, func=AF.Exp, scale=-1.0), reads=[Gf], writes=[Gf])
            op("act", lambda e: e.activation(out=Gf, in_=Gf, func=AF.Ln, bias=1.0, scale=1.0), reads=[Gf], writes=[Gf])
            for i in range(4):
                for d in range(2):
                    gv = G[:, i, d, :]
                    pc = next_acc()
                    op("pe", lambda e, pc=pc, gv=gv, d=d: e.matmul(pc[:], lhsT=tri[d], rhs=gv, start=True, stop=True), reads=[tri[d], gv], writes=[pc[:]])
                    ev = Etok[:, i, d, :]
                    op("act", lambda e, pc=pc, ev=ev: e.activation(out=ev, in_=pc[:], func=AF.Exp, scale=1.0 / 16.0), reads=[pc[:]], writes=[ev])
                    pcT = next_acc()
                    pv4 = pcT[:].rearrange("p (h t) -> p h t", t=128)
                    for h in range(4):
                        op("pe", lambda e, pv4=pv4, i=i, d=d, h=h: e.matmul(pv4[:, h, :], lhsT=G[:, i, d, h * 128:(h + 1) * 128], rhs=tri[d], start=True, stop=True),
                           reads=[G[:, i, d, h * 128:(h + 1) * 128], tri[d]], writes=[pv4[:, h, :]])
                    col = 63 if d == 0 else 0
                    ebv = ebc[:, d, :, 2 * i:2 * i + 2]
                    srcv = pv4.rearrange("p h (c l) -> p h c l", l=64)[:, :, :, col]
                    op("act", lambda e, ebv=ebv, srcv=srcv: e.activation(out=ebv, in_=srcv, func=AF.Exp, scale=-1.0 / 16.0), reads=[srcv], writes=[ebv])
                    if full:
                        o1 = EposT[:, d, :, i * 128:(i + 1) * 128]
                        o2 = EnegT[:, d, :, i * 128:(i + 1) * 128]
                        op("act", lambda e, pv4=pv4, o1=o1: e.activation(out=o1, in_=pv4, func=AF.Exp, scale=1.0 / 16.0), reads=[pv4], writes=[o1])
                        op("act", lambda e, pv4=pv4, o2=o2: e.activation(out=o2, in_=pv4, func=AF.Exp, scale=-1.0 / 16.0), reads=[pv4], writes=[o2])

        def projections(full):
            if full:
                s = wblock(w, 0, tag="gla_q")
                for h in range(4):
                    a = fm_matmul(s, h, lambda kc: cur["h"][:, kc, 0:512], 512)
                    for d in range(2):
                        op("dve", lambda e, a=a, d=d, h=h: e.scalar_tensor_tensor(out=qtil[:, d, h, :], in0=a[:], scalar=128.0 ** -0.5, in1=EnegT[:, d, h, :], op0=ALU.mult, op1=ALU.mult),
                           reads=[a[:], EnegT[:, d, h, :]], writes=[qtil[:, d, h, :]])
            s = wblock(w, 512, tag="gla_k" if full else None)
            if full:
                for h in range(4):
                    a = fm_matmul(s, h, lambda kc: cur["h"][:, kc, 0:512], 512)
                    for d in range(2):
                        op("dve", lambda e, a=a, d=d, h=h: e.tensor_tensor(out=ktilT[:, d, h, :], in0=a[:], in1=EposT[:, d, h, :], op=ALU.mult),
                           reads=[a[:], EposT[:, d, h, :]], writes=[ktilT[:, d, h, :]])
            for i in range(4):
                a = tile_proj(s, i)
                for d in range(2):
                    op("dve", lambda e, a=a, d=d, i=i: e.tensor_tensor(out=ktok[:, i, d, :], in0=a[:], in1=Etok[:, i, d, :], op=ALU.mult),
                       reads=[a[:], Etok[:, i, d, :]], writes=[ktok[:, i, d, :]])
            for hb in range(2):
                s = wblock(w, 1024 + hb * 512)
                for i in range(4):
                    a = tile_proj(s, i)
                    op("act", lambda e, a=a, i=i, hb=hb: e.copy(out=vt[:, i, hb * 512:(hb + 1) * 512], in_=a[:]), reads=[a[:]], writes=[vt[:, i, hb * 512:(hb + 1) * 512]])
            if full:
                for hb in range(2):
                    s = wblock(w, 2048 + hb * 512)
                    for fc in range(4):
                        a = fm_matmul(s, fc, lambda kc: cur["h"][:, kc, 0:512], 512)
                        op("act", lambda e, a=a, hb=hb, fc=fc: e.activation(out=ogT[:, hb * 4 + fc, :], in_=a[:], func=AF.Silu), reads=[a[:]], writes=[ogT[:, hb * 4 + fc, :]])

        def state_update(d, h, i, cpar):
            p0 = cpar * 64
            cidx = 2 * i + cpar
            pU = next_acc()
            op("pe", lambda e, pU=pU: e.matmul(pU[:, 0:256], lhsT=ktok[p0:p0 + 64, i, d, h * 128:(h + 1) * 128], rhs=vt[p0:p0 + 64, i, h * 256:(h + 1) * 256], start=True, stop=True),
               reads=[ktok[p0:p0 + 64, i, d, h * 128:(h + 1) * 128], vt[p0:p0 + 64, i, h * 256:(h + 1) * 256]], writes=[pU[:, 0:256]])
            eb = ebc[:, d, h, cidx:cidx + 1]
            sv = S[:, h, :]
            op("dve", lambda e: e.tensor_scalar(out=sv, in0=sv, scalar1=eb, scalar2=None, op0=ALU.mult), reads=[sv, eb], writes=[sv])
            op("dve", lambda e, pU=pU: e.scalar_tensor_tensor(out=sv, in0=pU[:, 0:256], scalar=eb, in1=sv, op0=ALU.mult, op1=ALU.add), reads=[pU[:, 0:256], eb, sv], writes=[sv])
            op("act", lambda e: e.copy(out=Sbf[:, h, :], in_=sv), reads=[sv], writes=[Sbf[:, h, :]])
            return eb

        def init_state(init):
            if init is None:
                op("dve", lambda e: e.memset(S[:], 0.0), writes=[S[:]])
            else:
                ld(S[:], init.rearrange("(h p) e -> p h e", p=128))
            op("act", lambda e: e.copy(out=Sbf[:], in_=S[:]), reads=[S[:]], writes=[Sbf[:]])

        def scan(d, tiles, init, full, track_prod):
            init_state(init)
            if track_prod:
                op("dve", lambda e: e.memset(pdp[:, d, :], 1.0), writes=[pdp[:, d, :]])
            order = tiles if d == 0 else tiles[::-1]
            for i in order:
                cols = slice(i * 128, (i + 1) * 128)
                cp_order = (0, 1) if d == 0 else (1, 0)
                for h in range(4):
                    if full:
                        pA = next_acc()
                        op("pe", lambda e, pA=pA, h=h, cols=cols: e.matmul(pA[:, 0:128], lhsT=ktilT[:, d, h, cols], rhs=qtil[:, d, h, cols], start=True, stop=True),
                           reads=[ktilT[:, d, h, cols], qtil[:, d, h, cols]], writes=[pA[:, 0:128]])
                        am = Am[h % 2]
                        op("dve", lambda e, pA=pA, am=am: e.tensor_tensor(out=am[:], in0=pA[:, 0:128], in1=tri[d], op=ALU.mult), reads=[pA[:, 0:128], tri[d]], writes=[am[:]])
                        pbanks = [PS[3], PS[4], PS[6], PS[7]]
                        pov = [pbanks[2 * (h % 2) + eh_][:, 0:128] for eh_ in range(2)]
                        for eh in range(2):
                            op("pe", lambda e, pov=pov, am=am, eh=eh, h=h, i=i: e.matmul(pov[eh], lhsT=vt[:, i, h * 256 + eh * 128:h * 256 + (eh + 1) * 128], rhs=am[:], start=True, stop=False, skip_group_check=True),
                               reads=[vt[:, i, h * 256 + eh * 128:h * 256 + (eh + 1) * 128], am[:]], writes=[pov[eh]])
                    for n_, cpar in enumerate(cp_order):
                        if full:
                            cc = slice(i * 128 + cpar * 64, i * 128 + cpar * 64 + 64)
                            for eh in range(2):
                                op("pe", lambda e, pov=pov, eh=eh, h=h, cc=cc, cpar=cpar, n_=n_: e.matmul(pov[eh][:, cpar * 64:cpar * 64 + 64], lhsT=Sbf[:, h, eh * 128:(eh + 1) * 128], rhs=qtil[:, d, h, cc], start=False, stop=(n_ == 1), skip_group_check=True),
                                   reads=[Sbf[:, h, eh * 128:(eh + 1) * 128], qtil[:, d, h, cc]], writes=[pov[eh][:, cpar * 64:cpar * 64 + 64]])
                        eb = state_update(d, h, i, cpar)
                        if track_prod:
                            pv_ = pdp[:, d, h:h + 1]
                            op("dve", lambda e, pv_=pv_, eb=eb: e.tensor_tensor(out=pv_, in0=pv_, in1=eb, op=ALU.mult), reads=[pv_, eb], writes=[pv_])
                    if full:
                        for eh in range(2):
                            ov = oT[:, 2 * h + eh, cols]
                            if d == 0:
                                op("act", lambda e, pov=pov, ov=ov, eh=eh: e.copy(out=ov, in_=pov[eh]), reads=[pov[eh]], writes=[ov])
                            else:
                                op("dve", lambda e, pov=pov, ov=ov, eh=eh: e.tensor_tensor(out=ov, in0=pov[eh], in1=ov, op=ALU.add), reads=[pov[eh], ov], writes=[ov])

        mod_transpose(xg[0], 0, 0, 1, HB[0])
        load_group(src, 1, xg[1])
        buf = xg[1]
        MARKS.append(("g.p1.modT", len(P.ops["pe"])))
        mod_transpose(buf, 1, 0, 1, HB[1])
        cur["h"] = HB[1]
        MARKS.append(("g.p1.gates", len(P.ops["pe"])))
        gates_and_decays(False)
        MARKS.append(("g.p1.proj", len(P.ops["pe"])))
        projections(False)
        MARKS.append(("g.p1.scan", len(P.ops["pe"])))
        dflat = gd_in.ap().rearrange("r c -> (r c)").rearrange("(d p h) -> d p h", d=2, p=128)
        for d in range(2):
            scan(d, [0, 1, 2, 3], None, False, True)
            stout(gl_in.ap()[d * 512:(d + 1) * 512, :].rearrange("(h p) e -> p h e", p=128), S[:], False)
            stout(dflat[d], pdp[:, d, :], False)
        wblock(w, 0, tag="gla_q", issue_only=True)
        wblock(w, 512, tag="gla_k", issue_only=True)
        allgather(gl_in, gl_out)
        allgather(gd_in, gd_out)
        T3 = carve(16384, [128, 3, 4, 256], F32)
        Ssel = carve(28672, [128, 4, 256], F32)
        dall = lnst[:, 0:32].rearrange("p (r d h) -> p r d h", r=4, d=2)

        def combine():
            ld(lnst[:, 0:32].rearrange("p (x h) -> p x h", h=4),
               gd_out.ap().rearrange("(r q) c -> r (q c)", q=4).rearrange("r (d p h) -> p (r d) h", d=2, p=128))
            for d in range(2):
                order = [0, 1, 2, 3] if d == 0 else [3, 2, 1, 0]
                ld(S[:], (s0f if d == 0 else s0b).rearrange("(h p) e -> p h e", p=128))
                for n_, r_ in enumerate(order[:3]):
                    ld(T3[:, n_], gl_out.ap()[r_ * 1024 + d * 512:r_ * 1024 + (d + 1) * 512, :].rearrange("(h p) e -> p h e", p=128))
                op("dve", lambda e: e.memset(Ssel[:], 0.0), writes=[Ssel[:]])
                for n_, r_ in enumerate(order):
                    for h in range(4):
                        op("dve", lambda e, h=h, r_=r_: e.scalar_tensor_tensor(out=Ssel[:, h, :], in0=S[:, h, :], scalar=oh4[:, r_:r_ + 1], in1=Ssel[:, h, :], op0=ALU.mult, op1=ALU.add),
                           reads=[S[:, h, :], oh4[:, r_:r_ + 1], Ssel[:, h, :]], writes=[Ssel[:, h, :]])
                    if n_ < 3:
                        for h in range(4):
                            op("dve", lambda e, h=h, d=d, r_=r_, n_=n_: e.scalar_tensor_tensor(out=S[:, h, :], in0=S[:, h, :], scalar=dall[:, r_, d, h:h + 1], in1=T3[:, n_, h, :], op0=ALU.mult, op1=ALU.add),
                               reads=[S[:, h, :], dall[:, r_, d, h:h + 1], T3[:, n_, h, :]], writes=[S[:, h, :]])
                stout(SIN[d].rearrange("(h p) e -> p h e", p=128), Ssel[:], False)

        load_group(src, 0, xg[0])
        for g in range(NG):
            buf = xg[g % 2]
            if g + 1 < NG:
                load_group(src, g + 1, xg[(g + 1) % 2])
            MARKS.append(("g.p2.g%d.comb" % g, len(P.ops["pe"])))
            MARKS.append(("g.p2.g%d.modT" % g, len(P.ops["pe"])))
            cur["h"] = HB[g]
            MARKS.append(("g.p2.g%d.gates" % g, len(P.ops["pe"])))
            gates_and_decays(True)
            MARKS.append(("g.p2.g%d.proj" % g, len(P.ops["pe"])))
            projections(True)
            if g == 1:
                combine()
            if g == 1 and not last:
                pre_ffn()
            MARKS.append(("g.p2.g%d.scan" % g, len(P.ops["pe"])))
            segs = [[0, 1], [2, 3]] if g == 0 else [[0, 1, 2, 3]]
            for d in range(2):
                if DEBUG[0] == 2 and d == 1:
                    continue
                for si, tl in enumerate(segs):
                    init = None if g == 0 else SIN[d]
                    scan(d, tl, init, True, False)
                    if g == 0:
                        outd = nsf if d == 0 else nsb
                        stout(outd[si * 512:(si + 1) * 512, :].rearrange("(h p) e -> p h e", p=128), S[:], True)
            if DEBUG[0] and g == 0:
                stout(DBG.rearrange("p (a t) -> p a t", t=512), oT[:], True)
                stout(DBGB[0].rearrange("p (a b t) -> p a b t", a=2, b=4), qtil[:], True)
                stout(DBGB[1].rearrange("p (a b t) -> p a b t", a=2, b=4), ktilT[:], True)
                stout(DBGB[2].rearrange("p (a t) -> p a t", a=4), vt[:], True)
                stout(DBGB[3].rearrange("p (a b t) -> p a b t", a=4, b=2), ktok[:], True)
            MARKS.append(("g.p2.g%d.fin" % g, len(P.ops["pe"])))
            rstd = sgt[1]
            for h in range(4):
                for eh in range(2):
                    op("act", lambda e, h=h, eh=eh: e.activation(out=sqt[:, eh, :], in_=oT[:, 2 * h + eh, :], func=AF.Square), reads=[oT[:, 2 * h + eh, :]], writes=[sqt[:, eh, :]])
                pr = next_acc()
                for eh in range(2):
                    op("pe", lambda e, pr=pr, eh=eh: e.matmul(pr[:], lhsT=onesb[:], rhs=sqt[:, eh, :], start=(eh == 0), stop=(eh == 1)), reads=[onesb[:], sqt[:, eh, :]], writes=[pr[:]])
                op("act", lambda e, pr=pr: e.activation(out=rstd[:], in_=pr[:], func=AF.Sqrt, bias=RMS_EPS, scale=1.0 / 256.0), reads=[pr[:]], writes=[rstd[:]])
                op("dve", lambda e: e.reciprocal(out=rstd[:], in_=rstd[:]), reads=[rstd[:]], writes=[rstd[:]])
                for eh in range(2):
                    ov = oT[:, 2 * h + eh, :]
                    op("dve", lambda e, ov=ov, eh=eh: e.scalar_tensor_tensor(out=ov, in0=ov, scalar=gnT[:, eh:eh + 1], in1=rstd[:], op0=ALU.mult, op1=ALU.mult), reads=[ov, gnT[:, eh:eh + 1], rstd[:]], writes=[ov])
                    op("dve", lambda e, ov=ov, h=h, eh=eh: e.tensor_tensor(out=oact[:, 2 * h + eh, :], in0=ov, in1=ogT[:, 2 * h + eh, :], op=ALU.mult), reads=[ov, ogT[:, 2 * h + eh, :]], writes=[oact[:, 2 * h + eh, :]])
            out_proj_epilogue(gla_w_o[j], 8, lambda kc, i: oact[:, kc, i * 128:(i + 1) * 128], buf, g, 0, dst, last)

    total_sub = 2 * nlayers if nsub is None else nsub
    sidx = 0
    MARKS.clear()
    MARKS.append(("prologue", 0))
    modulation_prologue()
    if stage is not None:
        if stage >= 1:
            modulation(0)
        if stage >= 2:
            load_group(xin, 1, xg[0])
            mod_transpose(xg[0], 1, 0, 1)
        if stage >= 3:
            load_ln(0, 0)
        op("dve", lambda e: e.tensor_copy(out=xg[1][:, 0, 0:96], in_=modT[:].rearrange("p a b -> p (a b)")), reads=[modT[:]], writes=[xg[1][:, 0, 0:96]])
        op("dve", lambda e: e.tensor_copy(out=xg[1][:, 1, :], in_=gbc[:, 1, 0, :]), reads=[gbc[:, 1, 0, :]], writes=[xg[1][:, 1, :]])
        op("dve", lambda e: e.tensor_copy(out=xg[1][:, 2, 0:512], in_=hT[:, 3, 0:512]), reads=[hT[:, 3, 0:512]], writes=[xg[1][:, 2, 0:512]])
        stout(y[0:512, :].rearrange("(i p) d -> p i d", p=128), xg[1][:], True)
        P.emit()
        return nc
    for l in range(nlayers):
        if sidx >= total_sub:
            break
        MARKS.append(("mod%d" % l, len(P.ops["pe"])))
        modulation(l)
        MARKS.append(("mixer%d" % l, len(P.ops["pe"])))
        kind = l % 3
        j = l // 3
        if kind == 0:
            conv_sublayer(l, j, sidx, sidx == total_sub - 1)
        elif kind == 1:
            attn_sublayer(l, j, sidx, sidx == total_sub - 1)
        else:
            gla_sublayer(l, j, sidx, sidx == total_sub - 1)
        sidx += 1
        if sidx >= total_sub:
            break
        MARKS.append(("ffn%d" % l, len(P.ops["pe"])))
        ffn_sublayer(l, sidx, sidx == total_sub - 1)
        sidx += 1
    MARKS.append(("end", len(P.ops["pe"])))
    P.emit()
    return nc


_CONST = {}


def _consts():
    if _CONST:
        return _CONST
    ident = np.eye(128, dtype=np.float32)
    rows = 2048 // 64
    row = np.repeat(np.arange(rows), 64)
    col = np.tile(np.arange(64), rows)
    pos = np.stack([row, col], -1).astype(np.float32)
    freqs = (10000.0 ** (-np.arange(32, dtype=np.float32) / 32)).astype(np.float32)
    ang = pos[:, :, None] * freqs
    s = np.arange(128)[:, None]
    t = np.arange(128)[None, :]
    same = (s // 64) == (t // 64)
    _CONST.update(ident=ident, rcos=np.cos(ang).reshape(2048, 64).astype(np.float32),
                  rsin=np.sin(ang).reshape(2048, 64).astype(np.float32),
                  trif=((s <= t) & same).astype(np.float32), trib=((s >= t) & same).astype(np.float32))
    return _CONST


_WNAMES = ["ln_g", "ln_b", "conv_w_in", "conv_w", "conv_w_out", "attn_w_qkv", "attn_q_norm",
           "attn_k_norm", "attn_w_o", "gla_w_in", "gla_w_gate1", "gla_w_gate2", "gla_b_gate", "gla_norm", "gla_w_o",
           "ffn_w_in", "ffn_w_out"]


def make_in_maps(inp, n_cores=8):
    c = _consts()
    f = lambda a: np.ascontiguousarray(np.asarray(a, dtype=np.float32))
    shared = {n: f(inp[n]) for n in _WNAMES}
    shared.update({kk: vv for kk, vv in c.items() if kk not in ("rcos", "rsin")})
    maps = []
    xp, xs = f(inp["x_prompt"]), f(inp["x_sample"])
    for r in range(n_cores):
        b, qi = r // 4, r % 4
        m = dict(shared)
        m["xin"] = np.ascontiguousarray(np.concatenate([xp[2 * r].reshape(256, D), xp[2 * r + 1].reshape(256, D),
                                                        xs[b, qi * 512:(qi + 1) * 512]], 0))
        m["cnd"] = np.ascontiguousarray(np.concatenate([f(inp["c_ctx"])[None, :], f(inp["c"])], 0))
        W_ = 6144 // 4
        m["w_ada_s"] = np.ascontiguousarray(f(inp["w_ada"])[:, :, qi * W_:(qi + 1) * W_])
        m["b_ada_s"] = np.ascontiguousarray(f(inp["b_ada"])[:, qi * W_:(qi + 1) * W_]).reshape(-1, 128)
        ohb_ = np.zeros((128, 2), np.float32)
        ohb_[:, b] = 1.0
        m["ohb"] = ohb_
        m["ck"] = f(inp["cache_k"])[b, 0].reshape(256, 256)
        m["cv"] = f(inp["cache_v"])[b, 0].reshape(256, 256)
        m["s0f"] = f(inp["state_gla_fwd"])[b, 0].reshape(512, 256)
        m["s0b"] = f(inp["state_gla_bwd"])[b, 0].reshape(512, 256)
        m["rcos"] = np.ascontiguousarray(c["rcos"][qi * 512:(qi + 1) * 512])
        m["rsin"] = np.ascontiguousarray(c["rsin"][qi * 512:(qi + 1) * 512])
        sel = np.zeros((8, 2), np.float32)
        if qi > 0:
            sel[2 * (qi - 1) + 1, 0] = 1.0
        if qi < 3:
            sel[2 * (qi + 1), 1] = 1.0
        m["sel"] = sel
        hf = np.zeros((128, 2), np.float32)
        hf[:, 0] = 1.0 if qi > 0 else 0.0
        hf[:, 1] = 1.0 if qi < 3 else 0.0
        m["hflag"] = hf
        oh = np.zeros((128, 4), np.float32)
        oh[:, qi] = 1.0
        m["oh4"] = oh
        maps.append(m)
    return maps


_NC = {}


def kernel(**inputs):
    if "nc" not in _NC:
        _NC["nc"] = build()
    nc = _NC["nc"]
    maps = make_in_maps(inputs)
    res = run_bass_kernel_spmd(nc, maps, core_ids=list(range(8)))
    R = res.results
    y_prompt = np.stack([R[r]["y"][s * 256:(s + 1) * 256] for r in range(8) for s in range(2)], 0)
    y_sample = np.stack([np.concatenate([R[4 * b + qi]["y"][512:1024] for qi in range(4)], 0) for b in range(2)], 0)
    nkk = np.stack([R[r]["nk"][s * 256:(s + 1) * 256].reshape(1, 256, 2, 128) for r in range(8) for s in range(2)], 0)
    nvv = np.stack([R[r]["nv"][s * 256:(s + 1) * 256].reshape(1, 256, 2, 128) for r in range(8) for s in range(2)], 0)
    sf = np.stack([R[r]["nsf"][s * 512:(s + 1) * 512].reshape(1, 4, 128, 256) for r in range(8) for s in range(2)], 0)
    sb = np.stack([R[r]["nsb"][s * 512:(s + 1) * 512].reshape(1, 4, 128, 256) for r in range(8) for s in range(2)], 0)
    return (y_prompt.astype(np.float32), y_sample.astype(np.float32), nkk.astype(np.float32), nvv.astype(np.float32),
            sf.astype(np.float32), sb.astype(np.float32))
```

```python
import contextlib
import numpy as np
import concourse.bass as bass
import concourse.mybir as mybir

F32 = mybir.dt.float32
BF16 = mybir.dt.bfloat16
AF = mybir.ActivationFunctionType
ALU = mybir.AluOpType
AX = mybir.AxisListType

ENGS = ("pe", "act", "dve", "pool", "sp")
STRICT = [True]
_DT_SIZE = {}


def _dsize(dt):
    s = str(dt)
    if "32" in s:
        return 4
    if "16" in s:
        return 2
    if "64" in s:
        return 8
    return 1


def region(ap):
    pairs = [tuple(x) for x in ap.ap]
    es = _dsize(ap.dtype)
    off = int(ap.offset)
    name = ap.name
    if str(ap.space) == "PSUM":
        return (name, 0, 128, 0, 1 << 20)
    if str(ap.space) in ("SB", "PSUM"):
        ps, pc = pairs[0]
        if ps == 0:
            ps = 1 << 40
        p0 = off // ps if ps < (1 << 40) else 0
        fo = off - p0 * ps if ps < (1 << 40) else off
        ext = 0
        for st, cnt in pairs[1:]:
            ext += abs(st) * (cnt - 1)
        return (name, p0, p0 + pc, fo * es, (fo + ext + 1) * es)
    ext = 0
    for st, cnt in pairs:
        ext += abs(st) * (cnt - 1)
    return (name, 0, 1, off * es, (off + ext + 1) * es)


class Op:
    __slots__ = ("eng", "fn", "deps", "flag", "k", "is_dma", "done", "idx", "is_cc")

    def __init__(self, eng, fn, is_dma):
        self.eng = eng
        self.fn = fn
        self.deps = []
        self.flag = False
        self.k = 0
        self.is_dma = is_dma
        self.done = None
        self.idx = 0
        self.is_cc = False


class Rec:
    __slots__ = ("p0", "p1", "lo", "hi", "writer", "readers", "pseudo")

    def __init__(self, p0, p1, lo, hi, writer, pseudo=False):
        self.p0, self.p1, self.lo, self.hi = p0, p1, lo, hi
        self.writer = writer
        self.readers = []
        self.pseudo = pseudo


class Prog:
    def __init__(self, nc, n_dma_sems=40):
        self.nc = nc
        self.ops = {e: [] for e in ENGS}
        self.recs = {}
        self.stack = contextlib.ExitStack()
        self.esem = {e: self.stack.enter_context(nc.semaphore("s_" + e)) for e in ENGS}
        self.dsems = [self.stack.enter_context(nc.semaphore("d%d" % i)) for i in range(n_dma_sems + 1)]
        self.cc_idx = n_dma_sems
        self.dval = [0] * (n_dma_sems + 1)
        self.dlast = [None] * (n_dma_sems + 1)
        self.n_dma = n_dma_sems
        self.dnext = 0
        self.dnext_pool = 0
        self.out_dmas = []
        self.nops = 0

    def sbuf(self, name, shape, dt):
        return self.stack.enter_context(self.nc.sbuf_tensor(name, list(shape), dt))

    def psum(self, name, shape, dt=F32):
        return self.stack.enter_context(self.nc.psum_tensor(name, list(shape), dt))

    def op(self, eng, fn, reads=(), writes=(), dma=False, is_out=False, same_ok=False, cc=False):
        o = Op(eng, fn, dma)
        o.idx = self.nops
        self.nops += 1
        deps = []
        reads = list(reads)
        writes = [(ap, False) for ap in writes]
        for ap in list(reads):
            if str(ap.space) == "PSUM":
                reads.remove(ap)
                writes.append((ap, True))
        for ap in reads:
            name, p0, p1, lo, hi = region(ap)
            for r in self.recs.get(name, ()):
                if r.p0 < p1 and p0 < r.p1 and r.lo < hi and lo < r.hi:
                    if r.writer is not None:
                        deps.append((r.writer, "raw"))
                    if not dma:
                        r.readers = [x for x in r.readers if x.is_dma or x.eng != eng]
                    r.readers.append(o)
        for ap, pseudo in writes:
            name, p0, p1, lo, hi = region(ap)
            lst = self.recs.setdefault(name, [])
            keep = []
            for r in lst:
                if r.p0 < p1 and p0 < r.p1 and r.lo < hi and lo < r.hi:
                    if r.writer is not None:
                        deps.append((r.writer, "rar" if (pseudo and r.pseudo) else ("raw" if pseudo else "waw")))
                    for x in r.readers:
                        if x is not o:
                            deps.append((x, "war"))
                    if p0 <= r.p0 and r.p1 <= p1 and lo <= r.lo and r.hi <= hi:
                        continue
                keep.append(r)
            keep.append(Rec(p0, p1, lo, hi, o, pseudo))
            self.recs[name] = keep
        for d, kind in deps:
            if d is o:
                continue
            if (not d.is_dma) and (not dma) and d.eng == eng:
                if eng == "pe" or same_ok or kind == "rar" or (kind != "raw" and not STRICT[0]):
                    continue
            o.deps.append(d)
            if not d.is_dma:
                d.flag = True
        if dma:
            half = self.n_dma // 2
            if cc:
                i = self.cc_idx
                o.is_cc = True
            elif eng == "pool":
                i = half + self.dnext_pool
                self.dnext_pool = (self.dnext_pool + 1) % (self.n_dma - half)
            else:
                i = self.dnext
                self.dnext = (self.dnext + 1) % half
            if self.dlast[i] is not None:
                o.deps.append(self.dlast[i])
            self.dval[i] += 1 if cc else 16
            o.done = (i, self.dval[i])
            self.dlast[i] = o
            if is_out:
                self.out_dmas.append(o)
        self.ops[eng].append(o)
        return o

    def emit(self):
        nc = self.nc
        for e in ENGS:
            k = 0
            for o in self.ops[e]:
                if o.flag and not o.is_dma:
                    k += 1
                    o.k = k
        final_waits = [o.done for o in self.out_dmas]
        engmap = {"pe": "tensor", "act": "scalar", "dve": "vector", "pool": "gpsimd", "sp": "sync"}
        with nc.Block() as block:
            for e in ENGS:
                def body(engine, e=e):
                    waited = {}
                    for o in self.ops[e]:
                        need = {}
                        for d in o.deps:
                            if d.is_dma:
                                key = ("d", d.done[0])
                                v = d.done[1]
                            else:
                                key = ("e", d.eng)
                                v = d.k
                            if v > need.get(key, 0):
                                need[key] = v
                        for key, v in need.items():
                            if waited.get(key, 0) >= v:
                                continue
                            waited[key] = v
                            sem = self.dsems[key[1]] if key[0] == "d" else self.esem[key[1]]
                            engine.wait_ge(sem, v)
                        ins = o.fn(engine)
                        if o.is_dma:
                            if o.is_cc:
                                ins.then_inc(self.dsems[o.done[0]])
                            else:
                                ins.then_inc(self.dsems[o.done[0]], 16)
                        elif o.flag:
                            ins.then_inc(self.esem[e], 1)
                    if e == "sp":
                        for (i, v) in final_waits:
                            if waited.get(("d", i), 0) < v:
                                waited[("d", i)] = v
                                engine.wait_ge(self.dsems[i], v)
                getattr(block, engmap[e])(body)
        self.stack.close()

from concourse.bass_utils import run_bass_kernel_spmd

D = 1024
DFF = 2816
NG = 2
NCORES = [8]
GROUPS = [[[0, 1, 2, 3], [4, 5, 6, 7]]]
TOK = NG * 512
DEPTH = 4
ALPHA = (2.0 * DEPTH) ** 0.25
LN_EPS = 1e-5
RMS_EPS = 1e-6
NSLOT = 6
DEBUG = [False]
MARKS = []


class K:
    pass


def build(nlayers=DEPTH, nsub=None, stage=None):
    nc = bass.Bass("TRN2", target_bir_lowering=False)
    k = K()

    def din(name, shape, dt=F32):
        return nc.dram_tensor(name, list(shape), dt, kind="ExternalInput").ap()

    def dout(name, shape, dt=F32):
        return nc.dram_tensor(name, list(shape), dt, kind="ExternalOutput").ap()

    xin = din("xin", [TOK, D])
    NC_ = 4
    CPC = 48 // NC_
    cnd = din("cnd", [3, D])
    ohb_d = din("ohb", [128, 2])
    ck = din("ck", [256, 256])
    cv = din("cv", [256, 256])
    s0f = din("s0f", [512, 256])
    s0b = din("s0b", [512, 256])
    w_ada_s = din("w_ada_s", [4, D, CPC * 128])
    b_ada_s = din("b_ada_s", [4 * CPC, 128])
    mod_in = nc.dram_tensor("mod_in", [128, 256], F32)
    mod_out = nc.dram_tensor("mod_out", [128 * NC_, 256], F32)
    ln_g = din("ln_g", [4, 2, D])
    ln_b = din("ln_b", [4, 2, D])
    conv_w_in = din("conv_w_in", [2, D, 3 * D])
    conv_w = din("conv_w", [2, 3, D])
    conv_w_out = din("conv_w_out", [2, D, D])
    attn_w_qkv = din("attn_w_qkv", [1, D, 1536])
    attn_q_norm = din("attn_q_norm", [1, 128])
    attn_k_norm = din("attn_k_norm", [1, 128])
    attn_w_o = din("attn_w_o", [1, D, D])
    gla_w_in = din("gla_w_in", [1, D, 3072])
    gla_w_gate1 = din("gla_w_gate1", [1, 2, D, 16])
    gla_w_gate2 = din("gla_w_gate2", [1, 2, 16, 512])
    gla_b_gate = din("gla_b_gate", [1, 2, 512])
    gla_norm = din("gla_norm", [1, 256])
    gla_w_o = din("gla_w_o", [1, D, D])
    ffn_w_in = din("ffn_w_in", [4, D, 2 * DFF])
    ffn_w_out = din("ffn_w_out", [4, DFF, D])
    ident_d = din("ident", [128, 128])
    rcos = din("rcos", [512, 64])
    rsin = din("rsin", [512, 64])
    sel_d = din("sel", [8, 2])
    hflag_d = din("hflag", [128, 2])
    oh4_d = din("oh4", [128, 4])
    trif_d = din("trif", [128, 128])
    trib_d = din("trib", [128, 128])

    y = dout("y", [TOK, D])
    nk = dout("nk", [512, 256])
    nv = dout("nv", [512, 256])
    nsf = dout("nsf", [1024, 256])
    nsb = dout("nsb", [1024, 256])

    XA = XB = None
    SIN = dout("SIN", [2, 512, 256])
    bnd_in = nc.dram_tensor("bnd_in", [2, D], F32)
    bnd_out = nc.dram_tensor("bnd_out", [8, D], F32)
    kv_in = nc.dram_tensor("kv_in", [128, 2048], BF16)
    kv_out = nc.dram_tensor("kv_out", [512, 2048], BF16)
    gl_in = nc.dram_tensor("gl_in", [1024, 256], F32)
    gl_out = nc.dram_tensor("gl_out", [4096, 256], F32)
    gd_in = nc.dram_tensor("gd_in", [4, 256], F32)
    gd_out = nc.dram_tensor("gd_out", [16, 256], F32)
    DBG = dout("DBG", [128, 4096]) if DEBUG[0] else None
    DBGB = [dout("DBGB%d" % i, [128, 4096], BF16) for i in range(4)] if DEBUG[0] else None

    P = Prog(nc, n_dma_sems=48)
    op = P.op

    xg = [P.sbuf("xg%d" % i, [128, 4, D], F32) for i in range(2)]
    xb0_ = P.sbuf("xb0", [128, D], BF16)
    xb = [xb0_, xb0_]
    misc1 = P.sbuf("misc1", [128, D], F32)
    cnds = misc1[0:3, :]
    misc2 = P.sbuf("misc2", [128, 512], F32)
    xh = misc1[0:2, :]
    xhb = misc2[0:2, :].bitcast(BF16)
    hT = P.sbuf("hT", [128, 8, 514], BF16)
    hT2 = P.sbuf("hT2", [128, 8, 512], BF16)
    slots = [P.sbuf("slot%d" % i, [128, 8, 512], BF16) for i in range(NSLOT)]
    ARENA_F = 16128
    arena = P.sbuf("arena", [128, ARENA_F], F32)
    gbc = P.sbuf("gbc", [128, 2, 2, D], F32)
    lnbc = P.sbuf("lnbc", [128, 2, D], F32)
    modT = P.sbuf("modT", [128, 48, 2], F32)
    badr = P.sbuf("badr", [48, 128], F32)
    silT = P.sbuf("silT", [128, 8, 3], BF16)
    modA = P.sbuf("modA", [128, 4, 144], F32)
    modpf = P.sbuf("modpf", [128, 256], F32)
    modp = modpf[:, 0:4 * CPC * 3].rearrange("p (a b) -> p a b", b=3)
    onesf = P.sbuf("onesf", [128, 128], F32)
    dg = [P.sbuf("dg%d" % i, [128, 128], F32) for i in range(2)]
    idf = P.sbuf("idf", [128, 128], F32)
    idb = P.sbuf("idb", [128, 128], BF16)
    onesb = P.sbuf("onesb", [128, 128], BF16)
    w1buf = P.sbuf("w1buf", [128, 8, 64], BF16)
    lnst = P.sbuf("lnst", [128, 64], F32)
    small = P.sbuf("small", [128, 72], F32)
    sgt = [P.sbuf("sgt%d" % i, [128, 512], F32) for i in range(2)]
    ttmp = [P.sbuf("ttmp%d" % i, [128, 512], F32) for i in range(2)]
    cwr = P.sbuf("cwr", [24, 128], F32)
    cwT = P.sbuf("cwT", [128, 24], F32)
    PS = [P.psum("ps%d" % i, [128, 512], F32) for i in range(8)]

    def carve(off_bytes, shape, dt):
        n = 1
        for s in shape[1:]:
            n *= s
        nb = n * _dsize(dt)
        assert off_bytes % 4 == 0 and nb % 4 == 0 and off_bytes + nb <= ARENA_F * 4, (off_bytes, nb)
        v = arena[:, off_bytes // 4:(off_bytes + nb) // 4]
        if dt != F32:
            v = v.bitcast(dt)
        if len(shape) == 3:
            v = v.rearrange("p (a b) -> p a b", b=shape[2])
        elif len(shape) == 4:
            v = v.rearrange("p (a b c) -> p a b c", b=shape[2], c=shape[3])
        return v

    st = {"slot": 0, "acc": 0, "sub": 0, "ffn_g0_ready": False}
    HB = [hT2, hT]
    cur = {"h": hT}

    def pre_ffn():
        mod_transpose(xg[0], 0, 3, 4, HB[0])
        st["ffn_g0_ready"] = True

    def next_slot():
        s = slots[st["slot"] % NSLOT]
        st["slot"] += 1
        return s

    def next_acc():
        a = PS[st["acc"] % 3]
        st["acc"] += 1
        return a

    def wload(dst, src):
        op("pool", lambda e: e.dma_start(out=dst, in_=src), reads=[src], writes=[dst], dma=True)

    pf = {}

    def wblock(wd, c0, ncols=512, r0=0, nkc=8, tag=None, issue_only=False):
        if tag is not None and tag in pf and not issue_only:
            return pf.pop(tag)
        s = next_slot()
        if issue_only:
            pf[tag] = s
        wload(s[:, 0:nkc, 0:ncols], wd[r0:r0 + nkc * 128, c0:c0 + ncols].rearrange("(kc p) c -> p kc c", p=128))
        return s

    def ld(dst, src):
        op("sp", lambda e: e.dma_start(out=dst, in_=src), reads=[src], writes=[dst], dma=True)

    def stout(dst, src, is_out):
        op("sp", lambda e: e.dma_start(out=dst, in_=src), reads=[src], writes=[dst], dma=True, is_out=is_out)

    def allgather(src_t, dst_t):
        op("pool", lambda e: e.collective_compute("AllGather", ALU.bypass, replica_groups=GROUPS[0],
                                                  ins=[src_t.ap().opt()], outs=[dst_t.ap().opt()]),
           reads=[src_t.ap()], writes=[dst_t.ap()], dma=True, cc=True)

    hfl = small[:, 58:60]
    oh4 = small[:, 60:64]
    ld(hfl, hflag_d)
    ld(oh4, oh4_d)
    ld(idf[:], ident_d)
    ld(cnds, cnd)
    op("act", lambda e: e.copy(out=idb[:], in_=idf[:]), reads=[idf[:]], writes=[idb[:]])
    op("dve", lambda e: e.memset(onesb[:], 1.0), writes=[onesb[:]])
    op("dve", lambda e: e.memset(w1buf[:], 0.0), writes=[w1buf[:]])
    ohb = small[:, 64:66]
    ld(ohb, ohb_d)
    op("dve", lambda e: e.memset(onesf[:], 1.0), writes=[onesf[:]])
    pt32 = PS[7][:, 0:24].rearrange("p (a b) -> p a b", b=3)
    for kc in range(8):
        op("pe", lambda e, kc=kc: e.transpose(out=pt32[:, kc, :], in_=cnds[:, kc * 128:(kc + 1) * 128], identity=idf[0:3, 0:3]),
           reads=[cnds[:, kc * 128:(kc + 1) * 128], idf[0:3, 0:3]], writes=[pt32[:, kc, :]])
    op("act", lambda e: e.activation(out=silT[:], in_=pt32, func=AF.Silu), reads=[pt32], writes=[silT[:]])

    def modulation_prologue():
        nb_ = 4 * CPC
        ld(badr[0:nb_, :], b_ada_s)
        pm = PS[7][:, 32:32 + nb_ * 3].rearrange("p (a b) -> p a b", b=3)
        for l in range(4):
            c0 = 0
            while c0 < CPC * 128:
                ncols = min(512, CPC * 128 - c0)
                s = next_slot()
                wload(s[:, :, 0:ncols], w_ada_s[l][:, c0:c0 + ncols].rearrange("(kc p) c -> p kc c", p=128))
                for fc in range(ncols // 128):
                    jc = c0 // 128 + fc
                    for kc in range(8):
                        op("pe", lambda e, s=s, fc=fc, kc=kc, l=l, jc=jc: e.matmul(pm[:, l * CPC + jc, :], lhsT=s[:, kc, fc * 128:(fc + 1) * 128],
                                                                                 rhs=silT[:, kc, :], start=(kc == 0), stop=(kc == 7)),
                           reads=[s[:, kc, fc * 128:(fc + 1) * 128], silT[:, kc, :]], writes=[pm[:, l * CPC + jc, :]])
                c0 += ncols
        pb = PS[6][:, 0:nb_]
        op("pe", lambda e: e.transpose(out=pb, in_=badr[0:nb_, :], identity=idf[0:nb_, 0:nb_]),
           reads=[badr[0:nb_, :], idf[0:nb_, 0:nb_]], writes=[pb])
        bT = sgt[0][:, 0:nb_]
        op("act", lambda e: e.copy(out=bT, in_=pb), reads=[pb], writes=[bT])
        op("dve", lambda e: e.memset(modpf[:], 0.0), writes=[modpf[:]])
        op("dve", lambda e: e.tensor_tensor(out=modp, in0=pm, in1=bT.unsqueeze(2).to_broadcast([128, nb_, 3]), op=ALU.add),
           reads=[pm, bT], writes=[modp])
        stout(mod_in.ap(), modpf[:], False)
        for b_ in range(3):
            wblock(conv_w_in[0], b_ * 512, tag=("conv0", b_), issue_only=True)
        allgather(mod_in, mod_out)
        for r_ in range(NC_):
            ld(modA[:, :, r_ * CPC * 3:(r_ + 1) * CPC * 3], mod_out.ap()[r_ * 128:(r_ + 1) * 128, 0:4 * CPC * 3].rearrange("p (l x) -> p l x", l=4))

    def modulation(l):
        mv = modA[:, l, :].rearrange("p (j c) -> p j c", c=3)
        op("dve", lambda e: e.tensor_copy(out=modT[:, :, 0:1], in_=mv[:, :, 0:1]), reads=[mv], writes=[modT[:, :, 0:1]])
        op("dve", lambda e: e.tensor_scalar(out=modT[:, :, 1:2], in0=mv[:, :, 1:2], scalar1=ohb[:, 0:1], scalar2=None, op0=ALU.mult),
           reads=[mv, ohb[:, 0:1]], writes=[modT[:, :, 1:2]])
        op("dve", lambda e: e.scalar_tensor_tensor(out=modT[:, :, 1:2], in0=mv[:, :, 2:3], scalar=ohb[:, 1:2], in1=modT[:, :, 1:2], op0=ALU.mult, op1=ALU.add),
           reads=[mv, ohb[:, 1:2], modT[:, :, 1:2]], writes=[modT[:, :, 1:2]])
        n_ = 0
        for c in range(2):
            for sub, which in ((0, 2), (1, 5)):
                for half in range(2):
                    pg = next_acc()
                    for q in range(4):
                        kc = half * 4 + q
                        d_ = dg[n_ % 2]
                        n_ += 1
                        col = modT[:, which * 8 + kc, c:c + 1]
                        op("dve", lambda e, d_=d_, col=col: e.tensor_scalar(out=d_[:], in0=idf[:], scalar1=col, scalar2=None, op0=ALU.mult),
                           reads=[idf[:], col], writes=[d_[:]])
                        op("pe", lambda e, pg=pg, q=q, d_=d_: e.matmul(pg[:, q * 128:(q + 1) * 128], lhsT=onesf[:], rhs=d_[:], start=True, stop=True),
                           reads=[onesf[:], d_[:]], writes=[pg[:, q * 128:(q + 1) * 128]])
                    dst = gbc[:, c, sub, half * 512:(half + 1) * 512]
                    op("act", lambda e, pg=pg, dst=dst: e.copy(out=dst, in_=pg[:]), reads=[pg[:]], writes=[dst])
        for w in (1, 4):
            v = modT[:, w * 8:(w + 1) * 8, :]
            op("dve", lambda e, v=v: e.tensor_scalar_add(out=v, in0=v, scalar1=1.0), reads=[v], writes=[v])

    def x_src(sidx):
        return xin if sidx == 0 else (XA if sidx % 2 == 1 else XB)

    def x_dst(sidx, last):
        return y if last else (XA if sidx % 2 == 0 else XB)

    loaded = set()

    def load_group(src, g, buf):
        if src is not xin or g in loaded:
            return
        loaded.add(g)
        ld(buf[:], src[g * 512:(g + 1) * 512, :].rearrange("(i p) d -> p i d", p=128))

    def mod_transpose(buf, g, w_sh, w_sc, hT=hT):
        c = 0 if g == 0 else 1
        for i in range(4):
            ptb = PS[5 + (i % 2)][:].bitcast(BF16).rearrange("p (a b) -> p a b", b=128)
            xbt = xb[i % 2]
            op("dve", lambda e, i=i, xbt=xbt: e.tensor_copy(out=xbt[:], in_=buf[:, i, :]), reads=[buf[:, i, :]], writes=[xbt[:]])
            for kc in range(8):
                op("pe", lambda e, kc=kc, xbt=xbt, ptb=ptb: e.transpose(out=ptb[:, kc, :], in_=xbt[:, kc * 128:(kc + 1) * 128], identity=idb[:]),
                   reads=[xbt[:, kc * 128:(kc + 1) * 128], idb[:]], writes=[ptb[:, kc, :]])
            for kc in range(8):
                dst = hT[:, kc, i * 128:(i + 1) * 128]
                shc = modT[:, w_sh * 8 + kc, c:c + 1]
                scc = modT[:, w_sc * 8 + kc, c:c + 1]
                if True:
                    op("act", lambda e, kc=kc, dst=dst, ptb=ptb, shc=shc, scc=scc: e.activation(out=dst, in_=ptb[:, kc, :], func=AF.Identity, bias=shc, scale=scc),
                       reads=[ptb[:, kc, :], shc, scc], writes=[dst])
                else:
                    op("dve", lambda e, kc=kc, dst=dst, ptb=ptb, shc=shc, scc=scc: e.tensor_scalar(out=dst, in0=ptb[:, kc, :], scalar1=scc, scalar2=shc, op0=ALU.mult, op1=ALU.add),
                       reads=[ptb[:, kc, :], shc, scc], writes=[dst])

    def fm_matmul(s, fc, rhs_fn, n, nkc=8):
        a = next_acc()
        for kc in range(nkc):
            r = rhs_fn(kc)
            op("pe", lambda e, a=a, s=s, fc=fc, kc=kc, r=r: e.matmul(a[:, 0:n], lhsT=s[:, kc, fc * 128:(fc + 1) * 128], rhs=r,
                                                                     start=(kc == 0), stop=(kc == nkc - 1)),
               reads=[s[:, kc, fc * 128:(fc + 1) * 128], r], writes=[a[:, 0:n]])
        return a

    def load_ln(l, which):
        ld(lnbc[:, 0, :], ln_g[l, which].partition_broadcast(128))
        ld(lnbc[:, 1, :], ln_b[l, which].partition_broadcast(128))

    def out_proj_epilogue(wd, nkc, lhsT_fn, buf, g, sub, dst, is_out):
        out_proj_multi(wd, nkc, [(g, buf, lhsT_fn)], sub, dst, is_out)

    def out_proj_multi(wd, nkc, items, sub, dst, is_out):
        nb = (nkc + 7) // 8
        blocks = []
        for half in range(2):
            bl = []
            for b in range(nb):
                kk = min(8, nkc - b * 8)
                bl.append(wblock(wd, half * 512, 512, r0=b * 1024, nkc=kk))
            blocks.append(bl)
        cnt = 0
        for (g, buf, lhsT_fn) in items:
            c = 0 if g == 0 else 1
            for half in range(2):
                for i in range(4):
                    pz = (PS[3], PS[4], PS[7])[cnt % 3]
                    tt = ttmp[cnt % 2]
                    cnt += 1
                    for kc in range(nkc):
                        s = blocks[half][kc // 8]
                        lt = lhsT_fn(kc, i)
                        op("pe", lambda e, pz=pz, s=s, kc=kc, lt=lt: e.matmul(pz[:], lhsT=lt, rhs=s[:, kc % 8, :], start=(kc == 0), stop=(kc == nkc - 1)),
                           reads=[lt, s[:, kc % 8, :]], writes=[pz[:]])
                    xs = buf[:, i, half * 512:(half + 1) * 512]
                    gv = gbc[:, c, sub, half * 512:(half + 1) * 512]
                    op("dve", lambda e, pz=pz, tt=tt, gv=gv: e.tensor_tensor(out=tt[:], in0=pz[:], in1=gv, op=ALU.mult),
                       reads=[pz[:], gv], writes=[tt[:]])
                    op("dve", lambda e, xs=xs, tt=tt: e.scalar_tensor_tensor(out=xs, in0=xs, scalar=ALPHA, in1=tt[:], op0=ALU.mult, op1=ALU.add),
                       reads=[xs, tt[:]], writes=[xs])
            T4 = range(4)
            sbs = [lnst[:, 16 * i:16 * i + 16] for i in T4]
            for i in T4:
                for h in range(2):
                    op("dve", lambda e, i=i, h=h, sb=sbs[i], buf=buf: e.bn_stats(out=sb[:, h * 6:(h + 1) * 6], in_=buf[:, i, h * 512:(h + 1) * 512]),
                       reads=[buf[:, i, h * 512:(h + 1) * 512]], writes=[sbs[i][:, h * 6:(h + 1) * 6]])
            for i in T4:
                op("dve", lambda e, sb=sbs[i]: e.bn_aggr(out=sb[:, 12:14], in_=sb[:, 0:12].rearrange("p (a b) -> p a b", b=6)), reads=[sbs[i][:, 0:12]], writes=[sbs[i][:, 12:14]])
            for i in T4:
                op("act", lambda e, sb=sbs[i]: e.activation(out=sb[:, 14:15], in_=sb[:, 13:14], func=AF.Sqrt, bias=LN_EPS, scale=1.0),
                   reads=[sbs[i][:, 13:14]], writes=[sbs[i][:, 14:15]])
            for i in T4:
                op("dve", lambda e, sb=sbs[i]: e.reciprocal(out=sb[:, 14:15], in_=sb[:, 14:15]), reads=[sbs[i][:, 14:15]], writes=[sbs[i][:, 14:15]])
            for i in T4:
                op("dve", lambda e, sb=sbs[i]: e.tensor_scalar(out=sb[:, 15:16], in0=sb[:, 12:13], scalar1=sb[:, 14:15], scalar2=-1.0, op0=ALU.mult, op1=ALU.mult),
                   reads=[sbs[i][:, 12:13], sbs[i][:, 14:15]], writes=[sbs[i][:, 15:16]])
            for i in T4:
                z = buf[:, i, :]
                op("act", lambda e, z=z, sb=sbs[i]: e.activation(out=z, in_=z, func=AF.Identity, bias=sb[:, 15:16], scale=sb[:, 14:15]),
                   reads=[z, sbs[i][:, 14:16]], writes=[z])
            for i in T4:
                z = buf[:, i, :]
                op("dve", lambda e, z=z: e.tensor_tensor(out=z, in0=z, in1=lnbc[:, 0, :], op=ALU.mult), reads=[z, lnbc[:, 0, :]], writes=[z])
            for i in T4:
                z = buf[:, i, :]
                op("dve", lambda e, z=z: e.tensor_tensor(out=z, in0=z, in1=lnbc[:, 1, :], op=ALU.add), reads=[z, lnbc[:, 1, :]], writes=[z])
            if is_out:
                stout(dst[g * 512:(g + 1) * 512, :].rearrange("(i p) d -> p i d", p=128), buf[:], True)

    def ffn_sublayer(l, sidx, last):
        src, dst = x_src(sidx), x_dst(sidx, last)
        load_ln(l, 1)
        acts = [carve(0, [128, 22, 512], BF16), carve(22528, [128, 22, 512], BF16)]
        hTs = HB
        for g in range(NG):
            load_group(src, g, xg[g])
        if not st["ffn_g0_ready"]:
            mod_transpose(xg[0], 0, 3, 4, hTs[0])
        st["ffn_g0_ready"] = False
        wd = ffn_w_in[l]
        n_ = 0
        for j in range(11):
            s = next_slot()
            wload(s[:, :, 0:256], wd[:, j * 256:(j + 1) * 256].rearrange("(kc p) c -> p kc c", p=128))
            wload(s[:, :, 256:512], wd[:, DFF + j * 256:DFF + (j + 1) * 256].rearrange("(kc p) c -> p kc c", p=128))
            order = [(q, g) for q in range(2) for g in range(NG)] if j > 0 else [(0, 0), (1, 0), (0, 1), (1, 1)]
            for (q, g) in order:
                if j == 0 and (q, g) == (0, 1):
                    mod_transpose(xg[1], 1, 3, 4, hTs[1])
                if True:
                    hTg = hTs[g]
                    a = fm_matmul(s, q, lambda kc, hTg=hTg: hTg[:, kc, 0:512], 512)
                    sg = sgt[n_ % 2]
                    n_ += 1
                    op("act", lambda e, a=a, sg=sg: e.activation(out=sg[:], in_=a[:], func=AF.Silu), reads=[a[:]], writes=[sg[:]])
                    a2 = fm_matmul(s, 2 + q, lambda kc, hTg=hTg: hTg[:, kc, 0:512], 512)
                    dsta = acts[g][:, 2 * j + q, :]
                    op("dve", lambda e, a2=a2, sg=sg, dsta=dsta: e.tensor_tensor(out=dsta, in0=a2[:], in1=sg[:], op=ALU.mult),
                       reads=[a2[:], sg[:]], writes=[dsta])
        items = []
        for g in range(NG):
            actg = acts[g]
            items.append((g, xg[g], (lambda kc, i, actg=actg: actg[:, kc, i * 128:(i + 1) * 128])))
        out_proj_multi(ffn_w_out[l], 22, items, 1, dst, last)

    def conv_sublayer(l, j, sidx, last=False):
        src, dst = x_src(sidx), x_dst(sidx, last)
        load_ln(l, 0)
        ld(cwr[:], conv_w[j].rearrange("k (kc p) -> (k kc) p", p=128))
        pc = PS[7][:, 200:224]
        op("pe", lambda e: e.transpose(out=pc, in_=cwr[:], identity=idf[0:24, 0:24]), reads=[cwr[:], idf[0:24, 0:24]], writes=[pc])
        op("act", lambda e: e.copy(out=cwT[:], in_=pc), reads=[pc], writes=[cwT[:]])
        bgs = carve(0, [128, 8, 512], BF16)
        cgs = carve(8192, [128, 8, 512], F32)
        uu = carve(24576, [128, 8, 516], F32)
        cact = carve(41472, [128, 8, 512], BF16)
        cgh = carve(49664, [128, 8, 2], F32)
        wd = conv_w_in[j]
        load_group(src, 0, xg[0])
        load_group(src, 1, xg[1])
        if src is xin:
            stout(bnd_in.ap()[0:1, :], src[512:513, :], False)
            stout(bnd_in.ap()[1:2, :], src[1023:1024, :], False)
        else:
            stout(bnd_in.ap()[0:1, :], xg[1][0:1, 0, :], False)
            stout(bnd_in.ap()[1:2, :], xg[1][127:128, 3, :], False)
        allgather(bnd_in, bnd_out)
        ld(misc1[32:40, :], bnd_out.ap())
        ld(misc2[32:40, 0:2], sel_d)
        for hf_ in range(2):
            a_ = next_acc()
            op("pe", lambda e, a_=a_, hf_=hf_: e.matmul(a_[0:2, :], lhsT=misc2[32:40, 0:2], rhs=misc1[32:40, hf_ * 512:(hf_ + 1) * 512], start=True, stop=True),
               reads=[misc2[32:40, 0:2], misc1[32:40, hf_ * 512:(hf_ + 1) * 512]], writes=[a_[0:2, :]])
            op("act", lambda e, a_=a_, hf_=hf_: e.copy(out=misc1[0:2, hf_ * 512:(hf_ + 1) * 512], in_=a_[0:2, :]),
               reads=[a_[0:2, :]], writes=[misc1[0:2, hf_ * 512:(hf_ + 1) * 512]])
        for g in range(NG):
            buf = xg[g % 2]
            if g + 1 < NG:
                load_group(src, g + 1, xg[(g + 1) % 2])
            hg = HB[g]
            if g == 0:
                mod_transpose(buf, g, 0, 1, hg)
            has_prev = g >= 1
            has_next = g >= 1
            if g == 0:
                segs = [(0, 256, 1), (256, 256, 259)]
                zero_cols = [0, 257, 258, 515]
            else:
                segs = [(0, 512, 1)]
                zero_cols = ([] if has_prev else [0]) + ([] if has_next else [513])
            for zc in zero_cols:
                op("dve", lambda e, zc=zc: e.memset(uu[:, :, zc:zc + 1], 0.0), writes=[uu[:, :, zc:zc + 1]])
            nh = 0
            if has_prev or has_next:
                nh = 2
                op("dve", lambda e: e.tensor_copy(out=xhb, in_=xh), reads=[xh], writes=[xhb])
                pth = PS[5][:].bitcast(BF16)[:, 0:16].rearrange("p (a b) -> p a b", b=2)
                for kc in range(8):
                    op("pe", lambda e, kc=kc: e.transpose(out=pth[:, kc, :], in_=xhb[:, kc * 128:(kc + 1) * 128], identity=idb[0:2, 0:2]),
                       reads=[xhb[:, kc * 128:(kc + 1) * 128], idb[0:2, 0:2]], writes=[pth[:, kc, :]])
                for kc in range(8):
                    dsth = hT[:, kc, 512:514]
                    op("act", lambda e, kc=kc, dsth=dsth: e.activation(out=dsth, in_=pth[:, kc, :], func=AF.Identity,
                                                                       bias=modT[:, 0 * 8 + kc, 1:2], scale=modT[:, 1 * 8 + kc, 1:2]),
                       reads=[pth[:, kc, :], modT[:, kc, 1:2], modT[:, 8 + kc, 1:2]], writes=[dsth])
            ph = PS[7][:, 256:288].rearrange("p (a b) -> p a b", b=2)
            for blk in range(6):
                s = wblock(wd, blk * 512, tag=("conv%d" % l, blk) if g == 0 else None)
                for fc in range(4):
                    ch = (blk % 2) * 4 + fc
                    a = fm_matmul(s, fc, lambda kc, hg=hg: hg[:, kc, 0:512], 512)
                    if blk < 2:
                        op("act", lambda e, a=a, ch=ch: e.copy(out=bgs[:, ch, :], in_=a[:]), reads=[a[:]], writes=[bgs[:, ch, :]])
                    elif blk < 4:
                        op("act", lambda e, a=a, ch=ch: e.copy(out=cgs[:, ch, :], in_=a[:]), reads=[a[:]], writes=[cgs[:, ch, :]])
                    else:
                        for (c0, n, d0) in segs:
                            op("dve", lambda e, a=a, ch=ch, c0=c0, n=n, d0=d0: e.tensor_tensor(out=uu[:, ch, d0:d0 + n], in0=a[:, c0:c0 + n],
                                                                                               in1=cgs[:, ch, c0:c0 + n], op=ALU.mult),
                               reads=[a[:, c0:c0 + n], cgs[:, ch, c0:c0 + n]], writes=[uu[:, ch, d0:d0 + n]])
                    if nh and blk >= 2:
                        hidx = (blk - 2) * 4 + fc
                        for kc in range(8):
                            op("pe", lambda e, s=s, fc=fc, kc=kc, hidx=hidx: e.matmul(ph[:, hidx, :], lhsT=s[:, kc, fc * 128:(fc + 1) * 128],
                                                                                      rhs=hT[:, kc, 512:514], start=(kc == 0), stop=(kc == 7)),
                               reads=[s[:, kc, fc * 128:(fc + 1) * 128], hT[:, kc, 512:514]], writes=[ph[:, hidx, :]])
            if g == 1 and not last:
                pre_ffn()
            if nh:
                op("act", lambda e: e.copy(out=cgh[:], in_=ph[:, 0:8, :]), reads=[ph[:, 0:8, :]], writes=[cgh[:]])
                if has_prev:
                    op("dve", lambda e: e.tensor_tensor(out=uu[:, :, 0:1], in0=ph[:, 8:16, 0:1], in1=cgh[:, :, 0:1], op=ALU.mult),
                       reads=[ph[:, 8:16, 0:1], cgh[:, :, 0:1]], writes=[uu[:, :, 0:1]])
                if has_next:
                    op("dve", lambda e: e.tensor_tensor(out=uu[:, :, 513:514], in0=ph[:, 8:16, 1:2], in1=cgh[:, :, 1:2], op=ALU.mult),
                       reads=[ph[:, 8:16, 1:2], cgh[:, :, 1:2]], writes=[uu[:, :, 513:514]])
                op("dve", lambda e: e.tensor_scalar(out=uu[:, :, 0:1], in0=uu[:, :, 0:1], scalar1=hfl[:, 0:1], scalar2=None, op0=ALU.mult),
                   reads=[uu[:, :, 0:1], hfl[:, 0:1]], writes=[uu[:, :, 0:1]])
                op("dve", lambda e: e.tensor_scalar(out=uu[:, :, 513:514], in0=uu[:, :, 513:514], scalar1=hfl[:, 1:2], scalar2=None, op0=ALU.mult),
                   reads=[uu[:, :, 513:514], hfl[:, 1:2]], writes=[uu[:, :, 513:514]])
            items_ = [(ch, c0, n, d0, cgs[:, ch, c0:c0 + n]) for ch in range(8) for (c0, n, d0) in segs]
            for (ch, c0, n, d0, yv) in items_:
                op("dve", lambda e, ch=ch, n=n, d0=d0, yv=yv: e.tensor_scalar(out=yv, in0=uu[:, ch, d0:d0 + n], scalar1=cwT[:, 8 + ch:9 + ch], scalar2=None, op0=ALU.mult),
                   reads=[uu[:, ch, d0:d0 + n], cwT[:, 8 + ch:9 + ch]], writes=[yv])
            for (ch, c0, n, d0, yv) in items_:
                op("dve", lambda e, ch=ch, n=n, d0=d0, yv=yv: e.scalar_tensor_tensor(out=yv, in0=uu[:, ch, d0 - 1:d0 - 1 + n], scalar=cwT[:, ch:ch + 1], in1=yv, op0=ALU.mult, op1=ALU.add),
                   reads=[uu[:, ch, d0 - 1:d0 - 1 + n], cwT[:, ch:ch + 1], yv], writes=[yv])
            for (ch, c0, n, d0, yv) in items_:
                op("dve", lambda e, ch=ch, n=n, d0=d0, yv=yv: e.scalar_tensor_tensor(out=yv, in0=uu[:, ch, d0 + 1:d0 + 1 + n], scalar=cwT[:, 16 + ch:17 + ch], in1=yv, op0=ALU.mult, op1=ALU.add),
                   reads=[uu[:, ch, d0 + 1:d0 + 1 + n], cwT[:, 16 + ch:17 + ch], yv], writes=[yv])
            for (ch, c0, n, d0, yv) in items_:
                op("dve", lambda e, ch=ch, n=n, c0=c0, yv=yv: e.tensor_tensor(out=cact[:, ch, c0:c0 + n], in0=yv, in1=bgs[:, ch, c0:c0 + n], op=ALU.mult),
                   reads=[yv, bgs[:, ch, c0:c0 + n]], writes=[cact[:, ch, c0:c0 + n]])
            if g == 0:
                mod_transpose(xg[1], 1, 0, 1, HB[1])
            out_proj_epilogue(conv_w_out[j], 8, lambda kc, i: cact[:, kc, i * 128:(i + 1) * 128], buf, g, 0, dst, last)

    def attn_sublayer(l, j, sidx, last=False):
        src, dst = x_src(sidx), x_dst(sidx, last)
        load_ln(l, 0)
        qg = carve(0, [128, 128], F32)
        kg = carve(512, [128, 128], F32)
        KT = carve(1024, [128, 2, 2304], BF16)
        VA = carve(10240, [128, 18, 256], BF16)
        KTp = carve(19456, [128, 2, 512], BF16)
        Vp = carve(21504, [128, 4, 256], BF16)
        qT = carve(23552, [128, 8, 512], BF16)
        oT = carve(31744, [128, 8, 512], BF16)
        sq = carve(39936, [128, 1280], F32)
        qn = carve(45056, [128, 1280], F32)
        rt = carve(50176, [128, 2, 640], F32)
        qb = carve(55296, [128, 1280], BF16)
        cs = carve(57856, [128, 2, 64], F32)
        kvo = carve(58368, [128, 2, 256], F32)
        rden = carve(60416, [128, 512], F32)
        pTb = [carve(62464, [128, 512], BF16), carve(63488, [128, 512], BF16)]
        ss = small[:, 32:44]
        rs = small[:, 44:56]
        ld(qg, attn_q_norm[j].partition_broadcast(128))
        ld(kg, attn_k_norm[j].partition_broadcast(128))
        wqkv = attn_w_qkv[j]
        SCALE = 128.0 ** -0.5

        cs_all = carve(58368, [128, 4, 2, 64], F32)

        def load_rope():
            ld(cs_all[:, :, 0, :], rcos.rearrange("(t p) f -> p t f", p=128))
            ld(cs_all[:, :, 1, :], rsin.rearrange("(t p) f -> p t f", p=128))

        load_rope()

        def norm_rope_jobs(jobs):
            for J in jobs:
                H, c0 = J["H"], J["c0"]
                J["sqv"] = sq[:, c0:c0 + H * 128]
                J["qnv"] = qn[:, c0:c0 + H * 128]
                J["ssv"] = ss[:, J["so"]:J["so"] + H]
                J["rsv"] = rs[:, J["so"]:J["so"] + H]
            for J in jobs:
                op("act", lambda e, J=J: e.activation(out=J["sqv"], in_=J["pk"], func=AF.Square), reads=[J["pk"]], writes=[J["sqv"]])
            for J in jobs:
                op("dve", lambda e, J=J: e.tensor_reduce(out=J["ssv"], in_=J["sqv"].rearrange("p (h d) -> p h d", d=128), axis=AX.X, op=ALU.add),
                   reads=[J["sqv"]], writes=[J["ssv"]])
            for J in jobs:
                op("act", lambda e, J=J: e.activation(out=J["rsv"], in_=J["ssv"], func=AF.Sqrt, bias=RMS_EPS, scale=1.0 / 128.0), reads=[J["ssv"]], writes=[J["rsv"]])
            for J in jobs:
                op("dve", lambda e, J=J: e.reciprocal(out=J["rsv"], in_=J["rsv"]), reads=[J["rsv"]], writes=[J["rsv"]])
            for J in jobs:
                H = J["H"]
                q3 = J["qnv"].rearrange("p (h d) -> p h d", d=128)
                op("dve", lambda e, J=J, q3=q3, H=H: e.tensor_tensor(out=q3, in0=J["pk"].rearrange("p (h d) -> p h d", d=128),
                                                                 in1=J["rsv"].unsqueeze(2).to_broadcast([128, H, 128]), op=ALU.mult),
                   reads=[J["pk"], J["rsv"]], writes=[J["qnv"]])
            for J in jobs:
                H = J["H"]
                q3 = J["qnv"].rearrange("p (h d) -> p h d", d=128)
                op("dve", lambda e, J=J, q3=q3, H=H: e.tensor_tensor(out=q3, in0=q3, in1=J["gain"].unsqueeze(1).to_broadcast([128, H, 128]), op=ALU.mult),
                   reads=[J["qnv"], J["gain"]], writes=[J["qnv"]])
            ropes = []
            for J in jobs:
                H = J["H"]
                if J["ti"] is None:
                    op("act", lambda e, J=J: e.copy(out=J["out_bf"], in_=J["qnv"]), reads=[J["qnv"]], writes=[J["out_bf"]])
                    continue
                qv = J["qnv"].rearrange("p (h a t f) -> p h a t f", h=H, a=2, t=2, f=32)
                ob = J["out_bf"].rearrange("p (h a t f) -> p h a t f", h=H, a=2, t=2, f=32)
                cst = cs_all[:, J["ti"], :, :]
                R = dict(J=J, x1=qv[:, :, :, 0, :], x2=qv[:, :, :, 1, :], o1=ob[:, :, :, 0, :], o2=ob[:, :, :, 1, :],
                         cosb=cst[:, 0, :].rearrange("p (a f) -> p a f", a=2).unsqueeze(1).to_broadcast([128, H, 2, 32]),
                         sinb=cst[:, 1, :].rearrange("p (a f) -> p a f", a=2).unsqueeze(1).to_broadcast([128, H, 2, 32]),
                         ta=rt[:, 0, J["ro"]:J["ro"] + H * 64].rearrange("p (h a f) -> p h a f", h=H, a=2, f=32),
                         tb=rt[:, 1, J["ro"]:J["ro"] + H * 64].rearrange("p (h a f) -> p h a f", h=H, a=2, f=32),
                         rd=[J["qnv"], cst])
                ropes.append(R)
            for (dst_, a_, b_, opx) in (("ta", "x1", "cosb", ALU.mult), ("tb", "x2", "sinb", ALU.mult), ("o1", "ta", "tb", ALU.subtract),
                                        ("ta", "x2", "cosb", ALU.mult), ("tb", "x1", "sinb", ALU.mult), ("o2", "ta", "tb", ALU.add)):
                for R in ropes:
                    wr = [R["J"]["out_bf"]] if dst_ in ("o1", "o2") else [R[dst_]]
                    rd = [R["ta"], R["tb"]] if dst_ in ("o1", "o2") else R["rd"]
                    op("dve", lambda e, R=R, dst_=dst_, a_=a_, b_=b_, opx=opx: e.tensor_tensor(out=R[dst_], in0=R[a_], in1=R[b_], op=opx), reads=rd, writes=wr)

        def proj_tile(s, i, ncols=512):
            a = next_acc()
            hh = cur["h"]
            for kc in range(8):
                op("pe", lambda e, a=a, s=s, i=i, kc=kc, hh=hh: e.matmul(a[:, 0:ncols], lhsT=hh[:, kc, i * 128:(i + 1) * 128], rhs=s[:, kc, 0:ncols], start=(kc == 0), stop=(kc == 7)),
                   reads=[hh[:, kc, i * 128:(i + 1) * 128], s[:, kc, 0:ncols]], writes=[a[:, 0:ncols]])
            return a

        def transposes(srcb, nh, dstv, par):
            ptb = PS[5 + (par % 2)][:].bitcast(BF16).rearrange("p (a b) -> p a b", b=128)
            for h in range(nh):
                op("pe", lambda e, h=h, ptb=ptb: e.transpose(out=ptb[:, h, :], in_=srcb[:, h * 128:(h + 1) * 128], identity=idb[:]),
                   reads=[srcb[:, h * 128:(h + 1) * 128], idb[:]], writes=[ptb[:, h, :]])
            op("act", lambda e, ptb=ptb: e.copy(out=dstv, in_=ptb[:, 0:nh, :]), reads=[ptb[:, 0:nh, :]], writes=[dstv])

        mod_transpose(xg[0], 0, 0, 1, HB[0])
        ctmp = qb[:, 0:512].rearrange("p (t c) -> p t c", c=256)
        wload(ctmp, ck.rearrange("(t p) c -> p t c", p=128))
        wload(VA[:, 0:2, :], cv.rearrange("(t p) c -> p t c", p=128))
        for t in range(2):
            transposes(ctmp[:, t, :], 2, KT[:, :, t * 128:(t + 1) * 128], t)
        load_group(src, 1, xg[1])
        for g in range(1, NG):
            buf = xg[g % 2]
            if g + 1 < NG:
                load_group(src, g + 1, xg[(g + 1) % 2])
            mod_transpose(buf, g, 0, 1, HB[1])
            cur["h"] = HB[1]
            s2 = wblock(wqkv, 1024)
            for ip in (0, 2):
                pk2 = [proj_tile(s2, ip), proj_tile(s2, ip + 1)]
                jobs = [dict(pk=pk2[u][:, 0:256], H=2, gain=kg, ti=ip + u, out_bf=qb[:, u * 256:(u + 1) * 256], c0=u * 256, so=2 * u, ro=128 * u) for u in range(2)]
                norm_rope_jobs(jobs)
                for u in range(2):
                    ti = ip + u
                    transposes(qb[:, u * 256:(u + 1) * 256], 2, KT[:, :, 256 + ti * 128:256 + (ti + 1) * 128], u)
                    op("act", lambda e, pkv=pk2[u], ti=ti: e.copy(out=VA[:, 2 + ti, :], in_=pkv[:, 256:512]), reads=[pk2[u][:, 256:512]], writes=[VA[:, 2 + ti, :]])

        stout(kv_in.ap()[:, 0:1024].rearrange("p (a t) -> p a t", a=2), KT[:, :, 256:768], False)
        stout(kv_in.ap()[:, 1024:2048].rearrange("p (a c) -> p a c", a=4), VA[:, 2:6, :], False)
        for t_, c_ in (("aq0", 0), ("aq1", 512), ("akv", 1024)):
            wblock(wqkv, c_, tag=t_, issue_only=True)
        allgather(kv_in, kv_out)

        def load_gathered_kv():
            for r_ in range(4):
                ld(KT[:, :, 256 + r_ * 512:256 + (r_ + 1) * 512], kv_out.ap()[r_ * 128:(r_ + 1) * 128, 0:1024].rearrange("p (a t) -> p a t", a=2))
                ld(VA[:, 2 + r_ * 4:6 + r_ * 4, :], kv_out.ap()[r_ * 128:(r_ + 1) * 128, 1024:2048].rearrange("p (a c) -> p a c", a=4))

        def attend(n0, n, KTb, Vb, ktiles):
            nk_ = len(ktiles)
            for h in range(8):
                kv = h // 4
                po, pd = (PS[3], PS[4]) if h % 2 == 0 else (PS[6], PS[7])
                pTs = {}

                def score(idx):
                    kt = ktiles[idx]
                    ps_ = next_acc()
                    op("pe", lambda e, ps_=ps_, kt=kt, kv=kv, h=h: e.matmul(ps_[:, 0:n], lhsT=KTb[:, kv, kt * 128:(kt + 1) * 128], rhs=qT[:, h, n0:n0 + n], start=True, stop=True),
                       reads=[KTb[:, kv, kt * 128:(kt + 1) * 128], qT[:, h, n0:n0 + n]], writes=[ps_[:, 0:n]])
                    pT = pTb[idx % 2]
                    op("act", lambda e, ps_=ps_, pT=pT: e.activation(out=pT[:, 0:n], in_=ps_[:, 0:n], func=AF.Exp, scale=SCALE), reads=[ps_[:, 0:n]], writes=[pT[:, 0:n]])
                    pTs[idx] = pT

                def pv(idx):
                    kt = ktiles[idx]
                    pT = pTs[idx]
                    op("pe", lambda e, kt=kt, pT=pT, idx=idx, po=po, kv=kv: e.matmul(po[:, 0:n], lhsT=Vb[:, kt, kv * 128:(kv + 1) * 128], rhs=pT[:, 0:n], start=(idx == 0), stop=(idx == nk_ - 1)),
                       reads=[Vb[:, kt, kv * 128:(kv + 1) * 128], pT[:, 0:n]], writes=[po[:, 0:n]])
                    op("pe", lambda e, pT=pT, idx=idx, pd=pd: e.matmul(pd[:, 0:n], lhsT=onesb[:], rhs=pT[:, 0:n], start=(idx == 0), stop=(idx == nk_ - 1)),
                       reads=[onesb[:], pT[:, 0:n]], writes=[pd[:, 0:n]])

                score(0)
                for idx in range(nk_):
                    if idx + 1 < nk_:
                        score(idx + 1)
                    pv(idx)
                op("dve", lambda e, pd=pd: e.reciprocal(out=rden[:, 0:n], in_=pd[:, 0:n]), reads=[pd[:, 0:n]], writes=[rden[:, 0:n]])
                op("dve", lambda e, po=po, h=h: e.tensor_tensor(out=oT[:, h, n0:n0 + n], in0=po[:, 0:n], in1=rden[:, 0:n], op=ALU.mult),
                   reads=[po[:, 0:n], rden[:, 0:n]], writes=[oT[:, h, n0:n0 + n]])

        load_group(src, 0, xg[0])
        for g in range(NG):
            buf = xg[g % 2]
            if g + 1 < NG:
                load_group(src, g + 1, xg[(g + 1) % 2])
            if g == 1:
                load_gathered_kv()
                load_rope()
            cur["h"] = HB[g]
            s0 = wblock(wqkv, 0, tag="aq0")
            s1 = wblock(wqkv, 512, tag="aq1")
            s2 = wblock(wqkv, 1024, tag="akv") if g == 0 else None
            for i in range(4):
                ti = None if g == 0 else i
                pqs = [proj_tile(s0, i), proj_tile(s1, i)]
                jobs = [dict(pk=pqs[hf][:], H=4, gain=qg, ti=ti, out_bf=qb[:, hf * 512:(hf + 1) * 512], c0=hf * 512, so=4 * hf, ro=256 * hf) for hf in range(2)]
                if g == 0:
                    pkv = proj_tile(s2, i)
                    jobs.append(dict(pk=pkv[:, 0:256], H=2, gain=kg, ti=None, out_bf=qb[:, 1024:1280], c0=1024, so=8, ro=512))
                norm_rope_jobs(jobs)
                transposes(qb[:, 0:1024], 8, qT[:, :, i * 128:(i + 1) * 128], i)
                if g == 0:
                    transposes(qb[:, 1024:1280], 2, KTp[:, :, i * 128:(i + 1) * 128], i + 1)
                    op("act", lambda e, i=i: e.copy(out=kvo[:, 0, :], in_=qn[:, 1024:1280]), reads=[qn[:, 1024:1280]], writes=[kvo[:, 0, :]])
                    op("act", lambda e, pkv=pkv: e.copy(out=kvo[:, 1, :], in_=pkv[:, 256:512]), reads=[pkv[:, 256:512]], writes=[kvo[:, 1, :]])
                    op("act", lambda e, i=i: e.copy(out=Vp[:, i, :], in_=kvo[:, 1, :]), reads=[kvo[:, 1, :]], writes=[Vp[:, i, :]])
                    stout(nk[i * 128:(i + 1) * 128, :], kvo[:, 0, :], True)
                    stout(nv[i * 128:(i + 1) * 128, :], kvo[:, 1, :], True)
            if g == 0:
                for sidx_ in range(2):
                    attend(sidx_ * 256, 256, KTp, Vp, [2 * sidx_, 2 * sidx_ + 1])
            else:
                if not last:
                    pre_ffn()
                attend(0, 512, KT, VA, list(range(18)))
            out_proj_epilogue(attn_w_o[j], 8, lambda kc, i: oT[:, kc, i * 128:(i + 1) * 128], buf, g, 0, dst, last)

    def gla_sublayer(l, j, sidx, last=False):
        src, dst = x_src(sidx), x_dst(sidx, last)
        load_ln(l, 0)
        G = carve(0, [128, 4, 2, 512], F32)
        vt = carve(0, [128, 4, 1024], BF16)
        ogT = carve(8192, [128, 8, 512], BF16)
        Etok = carve(16384, [128, 4, 2, 512], BF16)
        EposT = carve(24576, [128, 2, 4, 512], BF16)
        oT = carve(16384, [128, 8, 512], F32)
        EnegT = carve(32768, [128, 2, 4, 512], BF16)
        ktok = carve(32768, [128, 4, 2, 512], BF16)
        qtil = carve(40960, [128, 2, 4, 512], BF16)
        oact = carve(40960, [128, 8, 512], BF16)
        ktilT = carve(49152, [128, 2, 4, 512], BF16)
        sqt = carve(49152, [128, 2, 512], BF16)
        S = carve(57344, [128, 4, 256], F32)
        Sbf = carve(61440, [128, 4, 256], BF16)
        ebc = carve(63488, [128, 2, 4, 8], F32)
        pdp = carve(63744, [128, 2, 4], F32)
        Am = [carve(63808, [128, 128], BF16), carve(64064, [128, 128], BF16)]
        tri = [misc1[:, 0:128], misc1[:, 128:256]]
        ld(tri[0], trif_d)
        ld(tri[1], trib_d)
        gnr = misc2[0:2, 0:128]
        ld(gnr, gla_norm[j].rearrange("(a p) -> a p", p=128))
        pgn = PS[7][:, 300:302]
        op("pe", lambda e: e.transpose(out=pgn, in_=gnr, identity=idf[0:2, 0:2]), reads=[gnr, idf[0:2, 0:2]], writes=[pgn])
        gnT = small[:, 56:58]
        op("act", lambda e: e.copy(out=gnT, in_=pgn), reads=[pgn], writes=[gnT])
        rTs = ttmp[0][0:64, 0:256].bitcast(BF16)
        w2s = sgt[0][0:64, 0:256].bitcast(BF16)
        bgb = [sgt[1], ttmp[1]]
        w = gla_w_in[j]
        for d in range(2):
            wload(w2s[d * 32:d * 32 + 16, :], gla_w_gate2[j, d])
            wload(w1buf[:, :, d * 32:d * 32 + 16], gla_w_gate1[j, d].rearrange("(kc p) r -> p kc r", p=128))

        def tile_proj(s, i, c0=0, ncols=512):
            a = next_acc()
            hh = cur["h"]
            for kc in range(8):
                op("pe", lambda e, a=a, s=s, i=i, kc=kc, hh=hh: e.matmul(a[:, 0:ncols], lhsT=hh[:, kc, i * 128:(i + 1) * 128], rhs=s[:, kc, c0:c0 + ncols], start=(kc == 0), stop=(kc == 7)),
                   reads=[hh[:, kc, i * 128:(i + 1) * 128], s[:, kc, c0:c0 + ncols]], writes=[a[:, 0:ncols]])
            return a

        def gates_and_decays(full):
            sg1 = w1buf
            for d in range(2):
                ld(bgb[d][:], gla_b_gate[j, d].partition_broadcast(128))
            pr = next_acc()
            hh = cur["h"]
            for kc in range(8):
                op("pe", lambda e, kc=kc, hh=hh: e.matmul(pr[0:64, :], lhsT=sg1[:, kc, 0:64], rhs=hh[:, kc, 0:512], start=(kc == 0), stop=(kc == 7)),
                   reads=[sg1[:, kc, 0:64], hh[:, kc, 0:512]], writes=[pr[0:64, :]])
            op("act", lambda e: e.copy(out=rTs, in_=pr[0:64, :]), reads=[pr[0:64, :]], writes=[rTs])
            for i in range(4):
                for d in range(2):
                    pz = next_acc()
                    op("pe", lambda e, pz=pz, i=i, d=d: e.matmul(pz[:], lhsT=rTs[d * 32:d * 32 + 16, i * 128:(i + 1) * 128], rhs=w2s[d * 32:d * 32 + 16, :], start=True, stop=True),
                       reads=[rTs[d * 32:d * 32 + 16, i * 128:(i + 1) * 128], w2s[d * 32:d * 32 + 16, :]], writes=[pz[:]])
                    gv = G[:, i, d, :]
                    op("dve", lambda e, pz=pz, gv=gv, d=d: e.tensor_tensor(out=gv, in0=pz[:], in1=bgb[d][:], op=ALU.add), reads=[pz[:], bgb[d][:]], writes=[gv])
            Gf = carve(0, [128, 4096], F32)
            op("act", lambda e: e.activation(out=Gf, in_=Gf, func=AF.Exp, scale=-1.0), reads=[Gf], writes=[Gf])
            op("act", lambda e: e.activation(out=Gf, in_=Gf, func=AF.Ln, bias=1.0, scale=1.0), reads=[Gf], writes=[Gf])
            for i in range(4):
                for d in range(2):
                    gv = G[:, i, d, :]
                    pc = next_acc()
                    op("pe", lambda e, pc=pc, gv=gv, d=d: e.matmul(pc[:], lhsT=tri[d], rhs=gv, start=True, stop=True), reads=[tri[d], gv], writes=[pc[:]])
                    ev = Etok[:, i, d, :]
                    op("act", lambda e, pc=pc, ev=ev: e.activation(out=ev, in_=pc[:], func=AF.Exp, scale=1.0 / 16.0), reads=[pc[:]], writes=[ev])
                    pcT = next_acc()
                    pv4 = pcT[:].rearrange("p (h t) -> p h t", t=128)
                    for h in range(4):
                        op("pe", lambda e, pv4=pv4, i=i, d=d, h=h: e.matmul(pv4[:, h, :], lhsT=G[:, i, d, h * 128:(h + 1) * 128], rhs=tri[d], start=True, stop=True),
                           reads=[G[:, i, d, h * 128:(h + 1) * 128], tri[d]], writes=[pv4[:, h, :]])
                    col = 63 if d == 0 else 0
                    ebv = ebc[:, d, :, 2 * i:2 * i + 2]
                    srcv = pv4.rearrange("p h (c l) -> p h c l", l=64)[:, :, :, col]
                    op("act", lambda e, ebv=ebv, srcv=srcv: e.activation(out=ebv, in_=srcv, func=AF.Exp, scale=-1.0 / 16.0), reads=[srcv], writes=[ebv])
                    if full:
                        o1 = EposT[:, d, :, i * 128:(i + 1) * 128]
                        o2 = EnegT[:, d, :, i * 128:(i + 1) * 128]
                        op("act", lambda e, pv4=pv4, o1=o1: e.activation(out=o1, in_=pv4, func=AF.Exp, scale=1.0 / 16.0), reads=[pv4], writes=[o1])
                        op("act", lambda e, pv4=pv4, o2=o2: e.activation(out=o2, in_=pv4, func=AF.Exp, scale=-1.0 / 16.0), reads=[pv4], writes=[o2])

        def projections(full):
            if full:
                s = wblock(w, 0, tag="gla_q")
                for h in range(4):
                    a = fm_matmul(s, h, lambda kc: cur["h"][:, kc, 0:512], 512)
                    for d in range(2):
                        op("dve", lambda e, a=a, d=d, h=h: e.scalar_tensor_tensor(out=qtil[:, d, h, :], in0=a[:], scalar=128.0 ** -0.5, in1=EnegT[:, d, h, :], op0=ALU.mult, op1=ALU.mult),
                           reads=[a[:], EnegT[:, d, h, :]], writes=[qtil[:, d, h, :]])
            s = wblock(w, 512, tag="gla_k" if full else None)
            if full:
                for h in range(4):
                    a = fm_matmul(s, h, lambda kc: cur["h"][:, kc, 0:512], 512)
                    for d in range(2):
                        op("dve", lambda e, a=a, d=d, h=h: e.tensor_tensor(out=ktilT[:, d, h, :], in0=a[:], in1=EposT[:, d, h, :], op=ALU.mult),
                           reads=[a[:], EposT[:, d, h, :]], writes=[ktilT[:, d, h, :]])
            for i in range(4):
                a = tile_proj(s, i)
                for d in range(2):
                    op("dve", lambda e, a=a, d=d, i=i: e.tensor_tensor(out=ktok[:, i, d, :], in0=a[:], in1=Etok[:, i, d, :], op=ALU.mult),
                       reads=[a[:], Etok[:, i, d, :]], writes=[ktok[:, i, d, :]])
            for hb in range(2):
                s = wblock(w, 1024 + hb * 512)
                for i in range(4):
                    a = tile_proj(s, i)
                    op("act", lambda e, a=a, i=i, hb=hb: e.copy(out=vt[:, i, hb * 512:(hb + 1) * 512], in_=a[:]), reads=[a[:]], writes=[vt[:, i, hb * 512:(hb + 1) * 512]])
            if full:
                for hb in range(2):
                    s = wblock(w, 2048 + hb * 512)
                    for fc in range(4):
                        a = fm_matmul(s, fc, lambda kc: cur["h"][:, kc, 0:512], 512)
                        op("act", lambda e, a=a, hb=hb, fc=fc: e.activation(out=ogT[:, hb * 4 + fc, :], in_=a[:], func=AF.Silu), reads=[a[:]], writes=[ogT[:, hb * 4 + fc, :]])

        def state_update(d, h, i, cpar):
            p0 = cpar * 64
            cidx = 2 * i + cpar
            pU = next_acc()
            op("pe", lambda e, pU=pU: e.matmul(pU[:, 0:256], lhsT=ktok[p0:p0 + 64, i, d, h * 128:(h + 1) * 128], rhs=vt[p0:p0 + 64, i, h * 256:(h + 1) * 256], start=True, stop=True),
               reads=[ktok[p0:p0 + 64, i, d, h * 128:(h + 1) * 128], vt[p0:p0 + 64, i, h * 256:(h + 1) * 256]], writes=[pU[:, 0:256]])
            eb = ebc[:, d, h, cidx:cidx + 1]
            sv = S[:, h, :]
            op("dve", lambda e: e.tensor_scalar(out=sv, in0=sv, scalar1=eb, scalar2=None, op0=ALU.mult), reads=[sv, eb], writes=[sv])
            op("dve", lambda e, pU=pU: e.scalar_tensor_tensor(out=sv, in0=pU[:, 0:256], scalar=eb, in1=sv, op0=ALU.mult, op1=ALU.add), reads=[pU[:, 0:256], eb, sv], writes=[sv])
            op("act", lambda e: e.copy(out=Sbf[:, h, :], in_=sv), reads=[sv], writes=[Sbf[:, h, :]])
            return eb

        def init_state(init):
            if init is None:
                op("dve", lambda e: e.memset(S[:], 0.0), writes=[S[:]])
            else:
                ld(S[:], init.rearrange("(h p) e -> p h e", p=128))
            op("act", lambda e: e.copy(out=Sbf[:], in_=S[:]), reads=[S[:]], writes=[Sbf[:]])

        def scan(d, tiles, init, full, track_prod):
            init_state(init)
            if track_prod:
                op("dve", lambda e: e.memset(pdp[:, d, :], 1.0), writes=[pdp[:, d, :]])
            order = tiles if d == 0 else tiles[::-1]
            for i in order:
                cols = slice(i * 128, (i + 1) * 128)
                cp_order = (0, 1) if d == 0 else (1, 0)
                for h in range(4):
                    if full:
                        pA = next_acc()
                        op("pe", lambda e, pA=pA, h=h, cols=cols: e.matmul(pA[:, 0:128], lhsT=ktilT[:, d, h, cols], rhs=qtil[:, d, h, cols], start=True, stop=True),
                           reads=[ktilT[:, d, h, cols], qtil[:, d, h, cols]], writes=[pA[:, 0:128]])
                        am = Am[h % 2]
                        op("dve", lambda e, pA=pA, am=am: e.tensor_tensor(out=am[:], in0=pA[:, 0:128], in1=tri[d], op=ALU.mult), reads=[pA[:, 0:128], tri[d]], writes=[am[:]])
                        pbanks = [PS[3], PS[4], PS[6], PS[7]]
                        pov = [pbanks[2 * (h % 2) + eh_][:, 0:128] for eh_ in range(2)]
                        for eh in range(2):
                            op("pe", lambda e, pov=pov, am=am, eh=eh, h=h, i=i: e.matmul(pov[eh], lhsT=vt[:, i, h * 256 + eh * 128:h * 256 + (eh + 1) * 128], rhs=am[:], start=True, stop=False, skip_group_check=True),
                               reads=[vt[:, i, h * 256 + eh * 128:h * 256 + (eh + 1) * 128], am[:]], writes=[pov[eh]])
                    for n_, cpar in enumerate(cp_order):
                        if full:
                            cc = slice(i * 128 + cpar * 64, i * 128 + cpar * 64 + 64)
                            for eh in range(2):
                                op("pe", lambda e, pov=pov, eh=eh, h=h, cc=cc, cpar=cpar, n_=n_: e.matmul(pov[eh][:, cpar * 64:cpar * 64 + 64], lhsT=Sbf[:, h, eh * 128:(eh + 1) * 128], rhs=qtil[:, d, h, cc], start=False, stop=(n_ == 1), skip_group_check=True),
                                   reads=[Sbf[:, h, eh * 128:(eh + 1) * 128], qtil[:, d, h, cc]], writes=[pov[eh][:, cpar * 64:cpar * 64 + 64]])
                        eb = state_update(d, h, i, cpar)
                        if track_prod:
                            pv_ = pdp[:, d, h:h + 1]
                            op("dve", lambda e, pv_=pv_, eb=eb: e.tensor_tensor(out=pv_, in0=pv_, in1=eb, op=ALU.mult), reads=[pv_, eb], writes=[pv_])
                    if full:
                        for eh in range(2):
                            ov = oT[:, 2 * h + eh, cols]
                            if d == 0:
                                op("act", lambda e, pov=pov, ov=ov, eh=eh: e.copy(out=ov, in_=pov[eh]), reads=[pov[eh]], writes=[ov])
                            else:
                                op("dve", lambda e, pov=pov, ov=ov, eh=eh: e.tensor_tensor(out=ov, in0=pov[eh], in1=ov, op=ALU.add), reads=[pov[eh], ov], writes=[ov])

        mod_transpose(xg[0], 0, 0, 1, HB[0])
        load_group(src, 1, xg[1])
        buf = xg[1]
        MARKS.append(("g.p1.modT", len(P.ops["pe"])))
        mod_transpose(buf, 1, 0, 1, HB[1])
        cur["h"] = HB[1]
        MARKS.append(("g.p1.gates", len(P.ops["pe"])))
        gates_and_decays(False)
        MARKS.append(("g.p1.proj", len(P.ops["pe"])))
        projections(False)
        MARKS.append(("g.p1.scan", len(P.ops["pe"])))
        dflat = gd_in.ap().rearrange("r c -> (r c)").rearrange("(d p h) -> d p h", d=2, p=128)
        for d in range(2):
            scan(d, [0, 1, 2, 3], None, False, True)
            stout(gl_in.ap()[d * 512:(d + 1) * 512, :].rearrange("(h p) e -> p h e", p=128), S[:], False)
            stout(dflat[d], pdp[:, d, :], False)
        wblock(w, 0, tag="gla_q", issue_only=True)
        wblock(w, 512, tag="gla_k", issue_only=True)
        allgather(gl_in, gl_out)
        allgather(gd_in, gd_out)
        T3 = carve(16384, [128, 3, 4, 256], F32)
        Ssel = carve(28672, [128, 4, 256], F32)
        dall = lnst[:, 0:32].rearrange("p (r d h) -> p r d h", r=4, d=2)

        def combine():
            ld(lnst[:, 0:32].rearrange("p (x h) -> p x h", h=4),
               gd_out.ap().rearrange("(r q) c -> r (q c)", q=4).rearrange("r (d p h) -> p (r d) h", d=2, p=128))
            for d in range(2):
                order = [0, 1, 2, 3] if d == 0 else [3, 2, 1, 0]
                ld(S[:], (s0f if d == 0 else s0b).rearrange("(h p) e -> p h e", p=128))
                for n_, r_ in enumerate(order[:3]):
                    ld(T3[:, n_], gl_out.ap()[r_ * 1024 + d * 512:r_ * 1024 + (d + 1) * 512, :].rearrange("(h p) e -> p h e", p=128))
                op("dve", lambda e: e.memset(Ssel[:], 0.0), writes=[Ssel[:]])
                for n_, r_ in enumerate(order):
                    for h in range(4):
                        op("dve", lambda e, h=h, r_=r_: e.scalar_tensor_tensor(out=Ssel[:, h, :], in0=S[:, h, :], scalar=oh4[:, r_:r_ + 1], in1=Ssel[:, h, :], op0=ALU.mult, op1=ALU.add),
                           reads=[S[:, h, :], oh4[:, r_:r_ + 1], Ssel[:, h, :]], writes=[Ssel[:, h, :]])
                    if n_ < 3:
                        for h in range(4):
                            op("dve", lambda e, h=h, d=d, r_=r_, n_=n_: e.scalar_tensor_tensor(out=S[:, h, :], in0=S[:, h, :], scalar=dall[:, r_, d, h:h + 1], in1=T3[:, n_, h, :], op0=ALU.mult, op1=ALU.add),
                               reads=[S[:, h, :], dall[:, r_, d, h:h + 1], T3[:, n_, h, :]], writes=[S[:, h, :]])
                stout(SIN[d].rearrange("(h p) e -> p h e", p=128), Ssel[:], False)

        load_group(src, 0, xg[0])
        for g in range(NG):
            buf = xg[g % 2]
            if g + 1 < NG:
                load_group(src, g + 1, xg[(g + 1) % 2])
            MARKS.append(("g.p2.g%d.comb" % g, len(P.ops["pe"])))
            MARKS.append(("g.p2.g%d.modT" % g, len(P.ops["pe"])))
            cur["h"] = HB[g]
            MARKS.append(("g.p2.g%d.gates" % g, len(P.ops["pe"])))
            gates_and_decays(True)
            MARKS.append(("g.p2.g%d.proj" % g, len(P.ops["pe"])))
            projections(True)
            if g == 1:
                combine()
            if g == 1 and not last:
                pre_ffn()
            MARKS.append(("g.p2.g%d.scan" % g, len(P.ops["pe"])))
            segs = [[0, 1], [2, 3]] if g == 0 else [[0, 1, 2, 3]]
            for d in range(2):
                if DEBUG[0] == 2 and d == 1:
                    continue
                for si, tl in enumerate(segs):
                    init = None if g == 0 else SIN[d]
                    scan(d, tl, init, True, False)
                    if g == 0:
                        outd = nsf if d == 0 else nsb
                        stout(outd[si * 512:(si + 1) * 512, :].rearrange("(h p) e -> p h e", p=128), S[:], True)
            if DEBUG[0] and g == 0:
                stout(DBG.rearrange("p (a t) -> p a t", t=512), oT[:], True)
                stout(DBGB[0].rearrange("p (a b t) -> p a b t", a=2, b=4), qtil[:], True)
                stout(DBGB[1].rearrange("p (a b t) -> p a b t", a=2, b=4), ktilT[:], True)
                stout(DBGB[2].rearrange("p (a t) -> p a t", a=4), vt[:], True)
                stout(DBGB[3].rearrange("p (a b t) -> p a b t", a=4, b=2), ktok[:], True)
            MARKS.append(("g.p2.g%d.fin" % g, len(P.ops["pe"])))
            rstd = sgt[1]
            for h in range(4):
                for eh in range(2):
                    op("act", lambda e, h=h, eh=eh: e.activation(out=sqt[:, eh, :], in_=oT[:, 2 * h + eh, :], func=AF.Square), reads=[oT[:, 2 * h + eh, :]], writes=[sqt[:, eh, :]])
                pr = next_acc()
                for eh in range(2):
                    op("pe", lambda e, pr=pr, eh=eh: e.matmul(pr[:], lhsT=onesb[:], rhs=sqt[:, eh, :], start=(eh == 0), stop=(eh == 1)), reads=[onesb[:], sqt[:, eh, :]], writes=[pr[:]])
                op("act", lambda e, pr=pr: e.activation(out=rstd[:], in_=pr[:], func=AF.Sqrt, bias=RMS_EPS, scale=1.0 / 256.0), reads=[pr[:]], writes=[rstd[:]])
                op("dve", lambda e: e.reciprocal(out=rstd[:], in_=rstd[:]), reads=[rstd[:]], writes=[rstd[:]])
                for eh in range(2):
                    ov = oT[:, 2 * h + eh, :]
                    op("dve", lambda e, ov=ov, eh=eh: e.scalar_tensor_tensor(out=ov, in0=ov, scalar=gnT[:, eh:eh + 1], in1=rstd[:], op0=ALU.mult, op1=ALU.mult), reads=[ov, gnT[:, eh:eh + 1], rstd[:]], writes=[ov])
                    op("dve", lambda e, ov=ov, h=h, eh=eh: e.tensor_tensor(out=oact[:, 2 * h + eh, :], in0=ov, in1=ogT[:, 2 * h + eh, :], op=ALU.mult), reads=[ov, ogT[:, 2 * h + eh, :]], writes=[oact[:, 2 * h + eh, :]])
            out_proj_epilogue(gla_w_o[j], 8, lambda kc, i: oact[:, kc, i * 128:(i + 1) * 128], buf, g, 0, dst, last)

    total_sub = 2 * nlayers if nsub is None else nsub
    sidx = 0
    MARKS.clear()
    MARKS.append(("prologue", 0))
    modulation_prologue()
    if stage is not None:
        if stage >= 1:
            modulation(0)
        if stage >= 2:
            load_group(xin, 1, xg[0])
            mod_transpose(xg[0], 1, 0, 1)
        if stage >= 3:
            load_ln(0, 0)
        op("dve", lambda e: e.tensor_copy(out=xg[1][:, 0, 0:96], in_=modT[:].rearrange("p a b -> p (a b)")), reads=[modT[:]], writes=[xg[1][:, 0, 0:96]])
        op("dve", lambda e: e.tensor_copy(out=xg[1][:, 1, :], in_=gbc[:, 1, 0, :]), reads=[gbc[:, 1, 0, :]], writes=[xg[1][:, 1, :]])
        op("dve", lambda e: e.tensor_copy(out=xg[1][:, 2, 0:512], in_=hT[:, 3, 0:512]), reads=[hT[:, 3, 0:512]], writes=[xg[1][:, 2, 0:512]])
        stout(y[0:512, :].rearrange("(i p) d -> p i d", p=128), xg[1][:], True)
        P.emit()
        return nc
    for l in range(nlayers):
        if sidx >= total_sub:
            break
        MARKS.append(("mod%d" % l, len(P.ops["pe"])))
        modulation(l)
        MARKS.append(("mixer%d" % l, len(P.ops["pe"])))
        kind = l % 3
        j = l // 3
        if kind == 0:
            conv_sublayer(l, j, sidx, sidx == total_sub - 1)
        elif kind == 1:
            attn_sublayer(l, j, sidx, sidx == total_sub - 1)
        else:
            gla_sublayer(l, j, sidx, sidx == total_sub - 1)
        sidx += 1
        if sidx >= total_sub:
            break
        MARKS.append(("ffn%d" % l, len(P.ops["pe"])))
        ffn_sublayer(l, sidx, sidx == total_sub - 1)
        sidx += 1
    MARKS.append(("end", len(P.ops["pe"])))
    P.emit()
    return nc


_CONST = {}


def _consts():
    if _CONST:
        return _CONST
    ident = np.eye(128, dtype=np.float32)
    rows = 2048 // 64
    row = np.repeat(np.arange(rows), 64)
    col = np.tile(np.arange(64), rows)
    pos = np.stack([row, col], -1).astype(np.float32)
    freqs = (10000.0 ** (-np.arange(32, dtype=np.float32) / 32)).astype(np.float32)
    ang = pos[:, :, None] * freqs
    s = np.arange(128)[:, None]
    t = np.arange(128)[None, :]
    same = (s // 64) == (t // 64)
    _CONST.update(ident=ident, rcos=np.cos(ang).reshape(2048, 64).astype(np.float32),
                  rsin=np.sin(ang).reshape(2048, 64).astype(np.float32),
                  trif=((s <= t) & same).astype(np.float32), trib=((s >= t) & same).astype(np.float32))
    return _CONST


_WNAMES = ["ln_g", "ln_b", "conv_w_in", "conv_w", "conv_w_out", "attn_w_qkv", "attn_q_norm",
           "attn_k_norm", "attn_w_o", "gla_w_in", "gla_w_gate1", "gla_w_gate2", "gla_b_gate", "gla_norm", "gla_w_o",
           "ffn_w_in", "ffn_w_out"]


def make_in_maps(inp, n_cores=8):
    c = _consts()
    f = lambda a: np.ascontiguousarray(np.asarray(a, dtype=np.float32))
    shared = {n: f(inp[n]) for n in _WNAMES}
    shared.update({kk: vv for kk, vv in c.items() if kk not in ("rcos", "rsin")})
    maps = []
    xp, xs = f(inp["x_prompt"]), f(inp["x_sample"])
    for r in range(n_cores):
        b, qi = r // 4, r % 4
        m = dict(shared)
        m["xin"] = np.ascontiguousarray(np.concatenate([xp[2 * r].reshape(256, D), xp[2 * r + 1].reshape(256, D),
                                                        xs[b, qi * 512:(qi + 1) * 512]], 0))
        m["cnd"] = np.ascontiguousarray(np.concatenate([f(inp["c_ctx"])[None, :], f(inp["c"])], 0))
        W_ = 6144 // 4
        m["w_ada_s"] = np.ascontiguousarray(f(inp["w_ada"])[:, :, qi * W_:(qi + 1) * W_])
        m["b_ada_s"] = np.ascontiguousarray(f(inp["b_ada"])[:, qi * W_:(qi + 1) * W_]).reshape(-1, 128)
        ohb_ = np.zeros((128, 2), np.float32)
        ohb_[:, b] = 1.0
        m["ohb"] = ohb_
        m["ck"] = f(inp["cache_k"])[b, 0].reshape(256, 256)
        m["cv"] = f(inp["cache_v"])[b, 0].reshape(256, 256)
        m["s0f"] = f(inp["state_gla_fwd"])[b, 0].reshape(512, 256)
        m["s0b"] = f(inp["state_gla_bwd"])[b, 0].reshape(512, 256)
        m["rcos"] = np.ascontiguousarray(c["rcos"][qi * 512:(qi + 1) * 512])
        m["rsin"] = np.ascontiguousarray(c["rsin"][qi * 512:(qi + 1) * 512])
        sel = np.zeros((8, 2), np.float32)
        if qi > 0:
            sel[2 * (qi - 1) + 1, 0] = 1.0
        if qi < 3:
            sel[2 * (qi + 1), 1] = 1.0
        m["sel"] = sel
        hf = np.zeros((128, 2), np.float32)
        hf[:, 0] = 1.0 if qi > 0 else 0.0
        hf[:, 1] = 1.0 if qi < 3 else 0.0
        m["hflag"] = hf
        oh = np.zeros((128, 4), np.float32)
        oh[:, qi] = 1.0
        m["oh4"] = oh
        maps.append(m)
    return maps


_NC = {}


def kernel(**inputs):
    if "nc" not in _NC:
        _NC["nc"] = build()
    nc = _NC["nc"]
    maps = make_in_maps(inputs)
    res = run_bass_kernel_spmd(nc, maps, core_ids=list(range(8)))
    R = res.results
    y_prompt = np.stack([R[r]["y"][s * 256:(s + 1) * 256] for r in range(8) for s in range(2)], 0)
    y_sample = np.stack([np.concatenate([R[4 * b + qi]["y"][512:1024] for qi in range(4)], 0) for b in range(2)], 0)
    nkk = np.stack([R[r]["nk"][s * 256:(s + 1) * 256].reshape(1, 256, 2, 128) for r in range(8) for s in range(2)], 0)
    nvv = np.stack([R[r]["nv"][s * 256:(s + 1) * 256].reshape(1, 256, 2, 128) for r in range(8) for s in range(2)], 0)
    sf = np.stack([R[r]["nsf"][s * 512:(s + 1) * 512].reshape(1, 4, 128, 256) for r in range(8) for s in range(2)], 0)
    sb = np.stack([R[r]["nsb"][s * 512:(s + 1) * 512].reshape(1, 4, 128, 256) for r in range(8) for s in range(2)], 0)
    return (y_prompt.astype(np.float32), y_sample.astype(np.float32), nkk.astype(np.float32), nvv.astype(np.float32),
            sf.astype(np.float32), sb.astype(np.float32))
```
